# Optimizing a Trainium2 kernel written in Bass

```python
import math
import jax, jax.numpy as jnp
from jax import lax
import numpy as np

D_MODEL = 1024
BATCH = 16
SEQ = 2048
DEPTH = 2

GDN_HEADS = 4
GDN_HEAD_DIM = 128
GDN_WIDTH = GDN_HEADS * GDN_HEAD_DIM
GDN_CONV = 4
GDN_CHUNK = 64
S5_GROUP = 16
S5_GROUPS = 32
S5_WIDTH = S5_GROUPS * S5_GROUP
S5_STATE = 64
DIFF_HEADS = 4
DIFF_HEAD_DIM = 64
DIFF_V_DIM = 2 * DIFF_HEAD_DIM
DIFF_WIDTH = DIFF_HEADS * DIFF_V_DIM
Q_BLOCK = 128
MIX_WIDTH = GDN_WIDTH + S5_WIDTH + DIFF_WIDTH
IN_SIZES = (GDN_WIDTH, GDN_WIDTH, GDN_WIDTH, GDN_WIDTH, GDN_HEADS, GDN_HEADS,
            S5_WIDTH,
            DIFF_HEADS * 2 * DIFF_HEAD_DIM, DIFF_HEADS * 2 * DIFF_HEAD_DIM, DIFF_WIDTH)
IN_WIDTH = sum(IN_SIZES)
D_FF = 2816
FFN_CONV = 3
EPS = 1e-6

kernel_name = 'hymba_style_gdn_s5_diffattn_convffn'


def rms_norm(x, w, eps=EPS):
    xf = x.astype(jnp.float32)
    y = xf * lax.rsqrt(jnp.mean(xf * xf, axis=-1, keepdims=True) + eps)
    return y.astype(x.dtype) * w


def l2norm(x, eps=1e-6):
    xf = x.astype(jnp.float32)
    return xf * lax.rsqrt(jnp.sum(xf * xf, axis=-1, keepdims=True) + eps)


def split_cols(t, sizes):
    offs = np.cumsum(sizes)[:-1].tolist()
    return jnp.split(t, offs, axis=-1)


def causal_dwconv(x, w):
    k = w.shape[0]
    return lax.conv_general_dilated(
        x, w[:, None, :].astype(x.dtype), window_strides=(1,), padding=[(k - 1, 0)],
        dimension_numbers=('NWC', 'WIO', 'NWC'), feature_group_count=x.shape[-1])


def _gdn_chunk_step(state, inp):
    q_n, k_n, u_n, w_n, g_n, attn_n = inp
    v_new = u_n - jnp.einsum('bhck,bhkv->bhcv', w_n, state)
    out = (jnp.einsum('bhck,bhkv->bhcv', q_n * jnp.exp(g_n)[..., None], state)
           + jnp.einsum('bhcs,bhsv->bhcv', attn_n, v_new))
    g_last = g_n[..., -1]
    state = (state * jnp.exp(g_last)[..., None, None]
             + jnp.einsum('bhck,bhcv->bhkv', k_n * jnp.exp(g_last[..., None] - g_n)[..., None], v_new))
    return state, out


def gated_delta_chunked(q, k, v, beta, g):
    b, l, h, dk = q.shape
    dv = v.shape[-1]
    c = GDN_CHUNK
    n = l // c

    def to_chunks(t):
        return t.reshape(b, n, c, h, -1).transpose(0, 3, 1, 2, 4)

    q = to_chunks(q * dk ** -0.5)
    k = to_chunks(k)
    v = to_chunks(v)
    beta = to_chunks(beta[..., None])[..., 0]
    gc = jnp.cumsum(to_chunks(g[..., None])[..., 0], axis=-1)
    incl = jnp.tril(jnp.ones((c, c), dtype=bool))
    strict = jnp.tril(jnp.ones((c, c), dtype=bool), -1)
    diff = gc[..., :, None] - gc[..., None, :]
    decay = jnp.where(incl, jnp.exp(jnp.where(incl, diff, 0.0)), 0.0)
    k_beta = k * beta[..., None]
    lower = jnp.where(strict, jnp.einsum('bhnck,bhnsk->bhncs', k_beta, k) * decay, 0.0)
    eye = jnp.eye(c, dtype=jnp.float32)
    t_inv = lax.linalg.triangular_solve(lower + eye, jnp.broadcast_to(eye, lower.shape),
                                        left_side=True, lower=True, unit_diagonal=True)
    u = t_inv @ (v * beta[..., None])
    w = t_inv @ (k_beta * jnp.exp(gc)[..., None])
    attn = jnp.where(incl, jnp.einsum('bhnck,bhnsk->bhncs', q, k) * decay, 0.0)
    xs = tuple(jnp.moveaxis(t, 2, 0) for t in (q, k, u, w, gc, attn))
    s0 = jnp.zeros((b, h, dk, dv), jnp.float32)
    _, out = lax.scan(_gdn_chunk_step, s0, xs)
    return out.transpose(1, 0, 3, 2, 4).reshape(b, l, h, dv)


def _linear_recurrence_combine(left, right):
    a_i, b_i = left
    a_j, b_j = right
    return a_j * a_i, a_j * b_i + b_j


def s5_mixer(u, lam_re, lam_im, log_dt, b_re, b_im, c_re, c_im, d_skip, w_glu, b_glu):
    f32 = jnp.float32
    b, l, _ = u.shape
    uf = u.astype(f32).reshape(b, l, S5_GROUPS, S5_GROUP)
    lam = lax.complex(jnp.minimum(lam_re.astype(f32), -1e-4), lam_im.astype(f32))
    dt = jnp.exp(log_dt.astype(f32))[:, None]
    lam_bar = jnp.exp(lam * dt)
    b_bar = ((lam_bar - 1.0) / lam)[..., None] * lax.complex(b_re.astype(f32), b_im.astype(f32))
    bu = jnp.einsum('gph,blgh->blgp', b_bar, uf.astype(jnp.complex64))
    a = jnp.broadcast_to(lam_bar, (1, l) + lam_bar.shape)
    _, states = lax.associative_scan(_linear_recurrence_combine, (a, bu), axis=1)
    c = lax.complex(c_re.astype(f32), c_im.astype(f32))
    y = (jnp.real(jnp.einsum('ghp,blgp->blgh', c, states))
         + d_skip.astype(f32).reshape(S5_GROUPS, S5_GROUP) * uf)
    y = jax.nn.gelu(y.reshape(b, l, S5_WIDTH))
    y = y * jax.nn.sigmoid(y @ w_glu.astype(f32) + b_glu.astype(f32))
    return y.astype(u.dtype)


def diff_attention(q, k, v, lam):
    b, l, h, _, dh = q.shape
    nb = l // Q_BLOCK
    qf = (q.astype(jnp.float32) * dh ** -0.5).reshape(b, nb, Q_BLOCK, h, 2, dh).swapaxes(0, 1)
    kf = k.astype(jnp.float32)
    vf = v.astype(jnp.float32)
    key_pos = jnp.arange(l)

    def block(args):
        q_blk, start = args
        s = jnp.einsum('bqhcd,bkhcd->bhcqk', q_blk, kf)
        causal = (start + jnp.arange(Q_BLOCK))[:, None] >= key_pos[None, :]
        p = jax.nn.softmax(jnp.where(causal, s, -jnp.inf), axis=-1)
        a = p[:, :, 0] - lam * p[:, :, 1]
        return jnp.einsum('bhqk,bkhe->bqhe', a, vf)

    starts = jnp.arange(nb, dtype=jnp.int32) * Q_BLOCK
    out = lax.map(block, (qf, starts))
    return out.swapaxes(0, 1).reshape(b, l, h, -1).astype(v.dtype)


def conv_glu_ffn(h, w_up, conv_w, conv_b, w_down):
    u = causal_dwconv(h @ w_up, conv_w) + conv_b
    gate, up = jnp.split(u, 2, axis=-1)
    return (jax.nn.silu(gate) * up) @ w_down


def setup_inputs(seed: int = 0) -> dict:
    key = jax.random.key(seed)
    ks = iter(jax.random.split(key, 40))
    f32 = jnp.float32

    def nrm(shape, scale):
        return jax.random.normal(next(ks), shape, f32) * scale

    def gain(shape):
        return 1.0 + 0.01 * jax.random.normal(next(ks), shape, f32)

    x = nrm((BATCH, SEQ, D_MODEL), 1.0)
    attn_norm_w = gain((DEPTH, D_MODEL))
    w_in = nrm((DEPTH, D_MODEL, IN_WIDTH), D_MODEL ** -0.5)
    gdn_conv_w = nrm((DEPTH, GDN_CONV, 3 * GDN_WIDTH), GDN_CONV ** -0.5)
    gdn_a_log = jnp.log(jax.random.uniform(next(ks), (DEPTH, GDN_HEADS), f32, 1.0, 16.0))
    gdn_dt = jnp.exp(jax.random.uniform(next(ks), (DEPTH, GDN_HEADS), f32, math.log(1e-3), math.log(1e-1)))
    gdn_dt_bias = gdn_dt + jnp.log(-jnp.expm1(-gdn_dt))
    gdn_norm_w = gain((DEPTH, GDN_HEAD_DIM))
    s5_lambda_re = -0.5 + nrm((DEPTH, S5_GROUPS, S5_STATE), 0.01)
    s5_lambda_im = jnp.pi * jnp.arange(S5_STATE, dtype=f32) + nrm((DEPTH, S5_GROUPS, S5_STATE), 0.01)
    s5_log_dt = jax.random.uniform(next(ks), (DEPTH, S5_GROUPS), f32, math.log(1e-3), math.log(1e-1))
    s5_b_re = nrm((DEPTH, S5_GROUPS, S5_STATE, S5_GROUP), (2 * S5_GROUP) ** -0.5)
    s5_b_im = nrm((DEPTH, S5_GROUPS, S5_STATE, S5_GROUP), (2 * S5_GROUP) ** -0.5)
    s5_c_re = nrm((DEPTH, S5_GROUPS, S5_GROUP, S5_STATE), S5_STATE ** -0.5)
    s5_c_im = nrm((DEPTH, S5_GROUPS, S5_GROUP, S5_STATE), S5_STATE ** -0.5)
    s5_d = nrm((DEPTH, S5_WIDTH), 1.0)
    s5_w_glu = nrm((DEPTH, S5_WIDTH, S5_WIDTH), S5_WIDTH ** -0.5)
    s5_b_glu = nrm((DEPTH, S5_WIDTH), 0.01)
    s5_norm_w = gain((DEPTH, S5_WIDTH))
    diff_lambda_q1 = nrm((DEPTH, DIFF_HEAD_DIM), 0.1)
    diff_lambda_k1 = nrm((DEPTH, DIFF_HEAD_DIM), 0.1)
    diff_lambda_q2 = nrm((DEPTH, DIFF_HEAD_DIM), 0.1)
    diff_lambda_k2 = nrm((DEPTH, DIFF_HEAD_DIM), 0.1)
    diff_norm_w = gain((DEPTH, DIFF_V_DIM))
    w_out = nrm((DEPTH, MIX_WIDTH, D_MODEL), MIX_WIDTH ** -0.5)
    ffn_norm_w = gain((DEPTH, D_MODEL))
    ffn_w_up = nrm((DEPTH, D_MODEL, 2 * D_FF), D_MODEL ** -0.5)
    ffn_conv_w = nrm((DEPTH, FFN_CONV, 2 * D_FF), FFN_CONV ** -0.5)
    ffn_conv_b = nrm((DEPTH, 2 * D_FF), 0.01)
    ffn_w_down = nrm((DEPTH, D_FF, D_MODEL), D_FF ** -0.5)
    final_norm_w = gain((D_MODEL,))
    return {'x': x, 'attn_norm_w': attn_norm_w, 'w_in': w_in, 'gdn_conv_w': gdn_conv_w,
            'gdn_a_log': gdn_a_log, 'gdn_dt_bias': gdn_dt_bias, 'gdn_norm_w': gdn_norm_w,
            's5_lambda_re': s5_lambda_re, 's5_lambda_im': s5_lambda_im, 's5_log_dt': s5_log_dt,
            's5_b_re': s5_b_re, 's5_b_im': s5_b_im, 's5_c_re': s5_c_re, 's5_c_im': s5_c_im,
            's5_d': s5_d, 's5_w_glu': s5_w_glu, 's5_b_glu': s5_b_glu, 's5_norm_w': s5_norm_w,
            'diff_lambda_q1': diff_lambda_q1, 'diff_lambda_k1': diff_lambda_k1,
            'diff_lambda_q2': diff_lambda_q2, 'diff_lambda_k2': diff_lambda_k2,
            'diff_norm_w': diff_norm_w, 'w_out': w_out, 'ffn_norm_w': ffn_norm_w,
            'ffn_w_up': ffn_w_up, 'ffn_conv_w': ffn_conv_w, 'ffn_conv_b': ffn_conv_b,
            'ffn_w_down': ffn_w_down, 'final_norm_w': final_norm_w}


def reference(x, attn_norm_w, w_in, gdn_conv_w, gdn_a_log, gdn_dt_bias, gdn_norm_w,
              s5_lambda_re, s5_lambda_im, s5_log_dt, s5_b_re, s5_b_im, s5_c_re, s5_c_im,
              s5_d, s5_w_glu, s5_b_glu, s5_norm_w, diff_lambda_q1, diff_lambda_k1,
              diff_lambda_q2, diff_lambda_k2, diff_norm_w, w_out, ffn_norm_w, ffn_w_up,
              ffn_conv_w, ffn_conv_b, ffn_w_down, final_norm_w):
    f32 = jnp.float32
    b, l, _ = x.shape
    for i in range(DEPTH):
        h = rms_norm(x, attn_norm_w[i])
        g_q, g_k, g_v, g_z, g_b, g_a, s_u, d_q, d_k, d_v = split_cols(h @ w_in[i], IN_SIZES)

        qkv = jax.nn.silu(causal_dwconv(jnp.concatenate([g_q, g_k, g_v], axis=-1), gdn_conv_w[i]))
        g_q, g_k, g_v = (t.reshape(b, l, GDN_HEADS, GDN_HEAD_DIM) for t in jnp.split(qkv, 3, axis=-1))
        beta = jax.nn.sigmoid(g_b.astype(f32))
        log_decay = (-jnp.exp(gdn_a_log[i].astype(f32))
                     * jax.nn.softplus(g_a.astype(f32) + gdn_dt_bias[i].astype(f32)))
        o_gdn = gated_delta_chunked(l2norm(g_q), l2norm(g_k), g_v.astype(f32), beta, log_decay)
        o_gdn = (rms_norm(o_gdn, gdn_norm_w[i].astype(f32))
                 * jax.nn.silu(g_z.astype(f32).reshape(b, l, GDN_HEADS, GDN_HEAD_DIM)))
        o_gdn = o_gdn.reshape(b, l, GDN_WIDTH).astype(x.dtype)

        o_s5 = rms_norm(s5_mixer(s_u, s5_lambda_re[i], s5_lambda_im[i], s5_log_dt[i], s5_b_re[i],
                                 s5_b_im[i], s5_c_re[i], s5_c_im[i], s5_d[i], s5_w_glu[i], s5_b_glu[i]),
                        s5_norm_w[i])

        lam_init = 0.8 - 0.6 * math.exp(-0.3 * i)
        lam = (jnp.exp(jnp.sum(diff_lambda_q1[i].astype(f32) * diff_lambda_k1[i].astype(f32)))
               - jnp.exp(jnp.sum(diff_lambda_q2[i].astype(f32) * diff_lambda_k2[i].astype(f32)))
               + lam_init)
        o_diff = diff_attention(d_q.reshape(b, l, DIFF_HEADS, 2, DIFF_HEAD_DIM),
                                d_k.reshape(b, l, DIFF_HEADS, 2, DIFF_HEAD_DIM),
                                d_v.reshape(b, l, DIFF_HEADS, DIFF_V_DIM), lam)
        o_diff = (rms_norm(o_diff, diff_norm_w[i]) * (1.0 - lam_init)).reshape(b, l, DIFF_WIDTH)

        x = x + jnp.concatenate([o_gdn, o_s5, o_diff], axis=-1) @ w_out[i]

        x = x + conv_glu_ffn(rms_norm(x, ffn_norm_w[i]), ffn_w_up[i], ffn_conv_w[i],
                             ffn_conv_b[i], ffn_w_down[i])
    return rms_norm(x, final_norm_w)
```

```python
import math
from contextlib import ExitStack
import numpy as np
import concourse.bass as bass
import concourse.mybir as mybir
from concourse.bass_utils import run_bass_kernel_spmd

F32 = mybir.dt.float32
BF16 = mybir.dt.bfloat16
AF = mybir.ActivationFunctionType
ALU = mybir.AluOpType
AX = mybir.AxisListType
EPS = 1e-6
DEPTH = 2
N_CORES = 8
R_GQ, R_GK, R_GV, R_GZ, R_SU, R_DQ, R_DK, R_BA = 0, 512, 1024, 1536, 2048, 2560, 3072, 3584
C_DV, C_BA = 3584, 4096
PROJ_ROWS = 3592


class Res:
    __slots__ = ("w", "r")

    def __init__(self):
        self.w = None
        self.r = {}


class Tl:
    def __init__(self, h):
        self.h = h
        self.r = Res()

    def __getitem__(self, k):
        return self.h[k]


class Sched:
    NDS = 40

    def __init__(self, nc):
        self.nc = nc
        self.E = dict(pe=nc.tensor, act=nc.scalar, dve=nc.vector, pool=nc.gpsimd, sp=nc.sync)
        self.sem = {k: nc.alloc_semaphore("s_" + k) for k in self.E}
        self.cnt = {k: 0 for k in self.E}
        self.seen = {k: {} for k in self.E}
        self.dsem = [nc.alloc_semaphore("d%d" % i) for i in range(self.NDS)]
        self.dcnt = [0] * self.NDS
        self.dnext = 0

    def _semh(self, key):
        return self.dsem[key[1]] if isinstance(key, tuple) else self.sem[key]

    def wait(self, e, tok):
        key, val = tok
        if val <= 0 or self.seen[e].get(key, 0) >= val:
            return
        self.E[e].wait_ge(self._semh(key), val)
        self.seen[e][key] = val

    def _deps(self, reads, writes):
        deps = {}
        for r in reads:
            if r.w is not None:
                k, v = r.w
                deps[k] = max(deps.get(k, 0), v)
        for w in writes:
            if w.w is not None:
                k, v = w.w
                deps[k] = max(deps.get(k, 0), v)
            for k, v in w.r.items():
                deps[k] = max(deps.get(k, 0), v)
        return deps

    def _commit(self, tok, reads, writes):
        k, v = tok
        for r in reads:
            r.r[k] = v
        for w in writes:
            w.w = tok
            w.r = {}

    def op(self, e, fn, reads=(), writes=()):
        reads = [t.r for t in reads]
        writes = [t.r for t in writes]
        for k, v in self._deps(reads, writes).items():
            self.wait(e, (k, v))
        ins = fn()
        if isinstance(ins, (list, tuple)):
            ins = ins[-1]
        self.cnt[e] += 1
        ins.then_inc(self.sem[e], 1)
        self._commit((e, self.cnt[e]), reads, writes)

    def dma(self, out, in_, reads=(), writes=(), q="sp", **kw):
        reads = [t.r for t in reads]
        writes = [t.r for t in writes]
        i = self.dnext
        self.dnext = (i + 1) % self.NDS
        key = ("d", i)
        self.wait(q, (key, self.dcnt[i]))
        for k, v in self._deps(reads, writes).items():
            self.wait(q, (k, v))
        self.dcnt[i] += 16
        self.E[q].dma_start(out=out, in_=in_, **kw).then_inc(self.dsem[i], 16)
        self._commit((key, self.dcnt[i]), reads, writes)

    def barrier(self):
        for e in self.E:
            for f in self.E:
                self.wait(e, (f, self.cnt[f]))
            for i in range(self.NDS):
                self.wait(e, (("d", i), self.dcnt[i]))

    def finish(self):
        for i in range(self.NDS):
            self.wait("sp", (("d", i), self.dcnt[i]))


class Bld:
    def __init__(self, NS, L, dbg=False, depth=DEPTH):
        self.NS, self.L, self.TOK, self.dbg, self.depth = NS, L, NS * L, dbg, depth
        self.nc = bass.Bass("TRN2", target_bir_lowering=False)
        self.S = Sched(self.nc)
        self.uid = 0
        self.d = {}

    def inp(self, name, shape, dt=F32):
        self.d[name] = self.nc.dram_tensor(name, list(shape), dt, kind="ExternalInput").ap()

    def scratch(self, name, shape, dt=F32, out=False):
        kind = "ExternalOutput" if (out or self.dbg) else "Internal"
        self.d[name] = self.nc.dram_tensor(name, list(shape), dt, kind=kind).ap()
        return self.d[name]

    def sb(self, es, name, shape, dt=F32):
        self.uid += 1
        return Tl(es.enter_context(self.nc.sbuf_tensor("%s_%d" % (name, self.uid), list(shape), dt)))

    def ps(self, es, name, shape=(128, 512), dt=F32):
        self.uid += 1
        return Tl(es.enter_context(self.nc.psum_tensor("%s_%d" % (name, self.uid), list(shape), dt)))

    def op(self, e, fn, reads=(), writes=()):
        self.S.op(e, fn, reads, writes)

    def dma(self, out, in_, reads=(), writes=(), q="sp", **kw):
        self.S.dma(out, in_, reads, writes, q, **kw)


def mm_acc(nc, out, pairs):
    n = len(pairs)
    ins = None
    for i, (a, b) in enumerate(pairs):
        ins = nc.tensor.matmul(out, a, b, start=(i == 0), stop=(i == n - 1))
    return ins


def rms_rstd(B, pss, rs, rstd, n, eps=EPS):
    nc = B.nc
    B.op("act", lambda: nc.scalar.activation(rs[:], pss[:], AF.Sqrt, bias=eps, scale=1.0 / n), [pss], [rs])
    B.op("dve", lambda: nc.vector.reciprocal(rstd[:], rs[:]), [rs], [rstd])


def load_norm_tile(B, es, x_src, nw, ones, tt, X, H, sq, pss, rs, rstd):
    nc = B.nc
    ts = slice(tt * 512, (tt + 1) * 512)
    xv = x_src.rearrange("(kc p) t -> p kc t", p=128)
    B.dma(X[:], xv[:, :, ts], writes=[X])
    B.op("act", lambda: nc.scalar.activation(sq[:], X[:], AF.Square), [X], [sq])
    B.op("pe", lambda: mm_acc(nc, pss[:], [(ones[:], sq[:, kc, :]) for kc in range(8)]), [ones, sq], [pss])
    rms_rstd(B, pss, rs, rstd, 1024.0)
    for kc in range(8):
        B.op("dve", lambda: nc.vector.scalar_tensor_tensor(out=H[:, kc, :], in0=X[:, kc, :], scalar=nw[:, kc:kc + 1],
                                                           in1=rstd[:], op0=ALU.mult, op1=ALU.mult), [X, nw, rstd], [H])


def st_inproj(B, l, x_src, es):
    nc = B.nc
    proj, dv_tm = B.d["proj"], B.d["dv_tm"]
    if True:
        wv = B.d["w_in"][l].rearrange("(kc p) n -> p kc n", p=128)
        W = [B.sb(es, "win", [128, 4104], BF16) for _ in range(8)]
        for kc in range(8):
            B.dma(W[kc][:], wv[:, kc, :], writes=[W[kc]], q="pool")
        nw = B.sb(es, "nw", [128, 8])
        B.dma(nw[:], B.d["attn_norm_w"][l], writes=[nw])
        ones = B.sb(es, "ones", [128, 128], BF16)
        B.op("pool", lambda: nc.gpsimd.memset(ones[:], 1.0), [], [ones])
        xt = [B.sb(es, "xt", [128, 8, 512]) for _ in range(2)]
        hT = [B.sb(es, "hT", [128, 8, 512], BF16) for _ in range(2)]
        sq = B.sb(es, "sq", [128, 8, 512], BF16)
        rs = B.sb(es, "rs", [128, 512]); rstd = B.sb(es, "rstd", [128, 512])
        pss = B.ps(es, "pss")
        pm = [B.ps(es, "pm") for _ in range(4)]
        ob = [B.sb(es, "ob", [128, 512]) for _ in range(3)] + [None]
        ob[3] = ob[0]
        obv = [B.sb(es, "obv", [128, 512], BF16) for _ in range(2)]
        k = 0
        for tt in range(B.TOK // 512):
            X, H = xt[tt % 2], hT[tt % 2]
            ts = slice(tt * 512, (tt + 1) * 512)
            load_norm_tile(B, es, x_src, nw, ones, tt, X, H, sq, pss, rs, rstd)
            for oc in range(29):
                P, O = pm[k % 4], ob[k % 4]
                m = 128 if oc < 28 else 8
                c0 = oc * 128 if oc < 28 else C_BA
                B.op("pe", lambda: mm_acc(nc, P[0:m, :], [(W[kc][:, c0:c0 + m], H[:, kc, :]) for kc in range(8)]), [H] + W, [P])
                B.op("act", lambda: nc.scalar.copy(O[0:m, :], P[0:m, :]), [P], [O])
                B.dma(proj[oc * 128:oc * 128 + m, ts], O[0:m, :], reads=[O])
                k += 1
                if oc % 2 == 1:
                    yield
            for tb in range(4):
                P, O = pm[k % 4], obv[tb % 2]
                B.op("pe", lambda: mm_acc(nc, P[:], [(H[:, kc, tb * 128:(tb + 1) * 128], W[kc][:, C_DV:C_DV + 512]) for kc in range(8)]), [H] + W, [P])
                B.op("act", lambda: nc.scalar.copy(O[:], P[:]), [P], [O])
                B.dma(dv_tm[tt * 512 + tb * 128: tt * 512 + (tb + 1) * 128, :], O[:], reads=[O])
                k += 1


def st_attn(B, l, es):
    nc = B.nc
    L, NS = B.L, B.NS
    NKB = L // 128
    proj, dv_tm, mixT = B.d["proj"], B.d["dv_tm"], B.d["mixT"]
    lam_init = 0.8 - 0.6 * math.exp(-0.3 * l)
    if True:
        lv = [B.sb(es, "lv", [128, 64]) for _ in range(4)]
        for i, nm in enumerate(["diff_lambda_q1", "diff_lambda_k1", "diff_lambda_q2", "diff_lambda_k2"]):
            B.dma(lv[i][:], B.d[nm][l].partition_broadcast(128), writes=[lv[i]])
        pr = B.sb(es, "pr", [128, 64]); s1 = B.sb(es, "s1", [128, 1]); s2 = B.sb(es, "s2", [128, 1])
        nlam = B.sb(es, "nlam", [128, 1])
        B.op("dve", lambda: nc.vector.tensor_tensor(pr[:], lv[0][:], lv[1][:], ALU.mult), [lv[0], lv[1]], [pr])
        B.op("dve", lambda: nc.vector.reduce_sum(s1[:], pr[:], AX.X), [pr], [s1])
        B.op("dve", lambda: nc.vector.tensor_tensor(pr[:], lv[2][:], lv[3][:], ALU.mult), [lv[2], lv[3]], [pr])
        B.op("dve", lambda: nc.vector.reduce_sum(s2[:], pr[:], AX.X), [pr], [s2])
        B.op("act", lambda: nc.scalar.activation(s1[:], s1[:], AF.Exp), [s1], [s1])
        B.op("act", lambda: nc.scalar.activation(s2[:], s2[:], AF.Exp), [s2], [s2])
        B.op("dve", lambda: nc.vector.tensor_tensor(nlam[:], s2[:], s1[:], ALU.subtract), [s1, s2], [nlam])
        B.op("dve", lambda: nc.vector.tensor_scalar(nlam[:], nlam[:], -lam_init, None, ALU.add), [nlam], [nlam])
        wrow = B.sb(es, "wrow", [128, 128])
        B.dma(wrow[:], B.d["diff_norm_w"][l].partition_broadcast(128), writes=[wrow])
        B.op("dve", lambda: nc.vector.tensor_scalar(wrow[:], wrow[:], 1.0 - lam_init, None, ALU.mult), [wrow], [wrow])
        onesf = B.sb(es, "onesf", [128, 128]); ident = B.sb(es, "ident", [128, 128])
        tri = B.sb(es, "tri", [128, 128], BF16)
        B.op("pool", lambda: nc.gpsimd.memset(onesf[:], 1.0), [], [onesf])
        B.op("pool", lambda: nc.gpsimd.affine_select(out=ident[:], in_=onesf[:], pattern=[[-1, 128]], compare_op=ALU.is_equal,
                                                     fill=0.0, base=0, channel_multiplier=1), [onesf], [ident])
        B.op("pool", lambda: nc.gpsimd.affine_select(out=tri[:], in_=onesf[:], pattern=[[1, 128]], compare_op=ALU.is_ge,
                                                     fill=0.0, base=0, channel_multiplier=-1), [onesf], [tri])
        Vs = [B.sb(es, "V", [128, NKB, 4, 132], BF16) for _ in range(min(2, NS))]
        qTs = [B.sb(es, "qT", [128, L], BF16) for _ in range(2)]; kTs = [B.sb(es, "kT", [128, L], BF16) for _ in range(2)]
        qst = B.sb(es, "qst", [128, L]); kst = B.sb(es, "kst", [128, L])
        psc = [B.ps(es, "psc", [128, 1024]) for _ in range(2)]
        pacc = [B.ps(es, "pacc") for _ in range(2)]
        ptr = B.ps(es, "ptr")
        Pt = [B.sb(es, "Pt", [128, 2, 512], BF16) for _ in range(2)]
        rc = B.sb(es, "rc", [128, 2]); o0 = B.sb(es, "o0", [128, 128]); a = B.sb(es, "a", [128, 128])
        junk = B.sb(es, "junk", [128, 128]); ssq = B.sb(es, "ssq", [128, 1]); rs = B.sb(es, "rs", [128, 1])
        rstd = B.sb(es, "rstd", [128, 1]); an = B.sb(es, "an", [128, 128])
        ot = [B.sb(es, "ot", [128, 512], BF16) for _ in range(2)]
        for s in range(NS):
            V = Vs[s % len(Vs)]
            B.op("pool", lambda: nc.gpsimd.memset(V[:], 1.0), [], [V])
            for hh in range(4):
                B.dma(V[:, :, hh, 0:128], dv_tm[s * L:(s + 1) * L, hh * 128:(hh + 1) * 128].rearrange("(kb p) e -> p kb e", p=128), writes=[V])
            for h in range(4):
                qT, kT = qTs[h % 2], kTs[h % 2]
                B.dma(qst[:], proj[R_DQ + h * 128:R_DQ + (h + 1) * 128, s * L:(s + 1) * L], writes=[qst])
                B.dma(kst[:], proj[R_DK + h * 128:R_DK + (h + 1) * 128, s * L:(s + 1) * L], writes=[kst])
                B.op("act", lambda: nc.scalar.copy(qT[:], qst[:]), [qst], [qT])
                B.op("act", lambda: nc.scalar.copy(kT[:], kst[:]), [kst], [kT])
                items = [(qb, g) for qb in range(NKB) for g in range(qb // 4 + 1)]

                def score(i):
                    qb, g = items[i]
                    kb0 = g * 4; nk = min(4, qb + 1 - kb0); Sp = psc[i % 2]
                    B.op("pe", lambda: [nc.tensor.matmul(Sp[:, c * 512 + j * 128:c * 512 + (j + 1) * 128], kT[c * 64:(c + 1) * 64, (kb0 + j) * 128:(kb0 + j + 1) * 128],
                                                         qT[c * 64:(c + 1) * 64, qb * 128:(qb + 1) * 128], start=True, stop=True) for c in (0, 1) for j in range(nk)],
                         [kT, qT], [Sp])

                def expmask(i):
                    qb, g = items[i]
                    kb0 = g * 4; nk = min(4, qb + 1 - kb0); Sp = psc[i % 2]; P = Pt[i % 2]
                    Spv = Sp[:, :].rearrange("p (c k) -> p c k", c=2)
                    B.op("act", lambda: nc.scalar.activation(P[:, :, 0:nk * 128], Spv[:, :, 0:nk * 128], AF.Exp, scale=0.125), [Sp], [P])
                    if kb0 + nk - 1 == qb:
                        B.op("dve", lambda: nc.vector.tensor_tensor(P[:, :, (nk - 1) * 128:nk * 128], P[:, :, (nk - 1) * 128:nk * 128],
                                                                    tri[:].unsqueeze(1).to_broadcast([128, 2, 128]), ALU.mult), [P, tri], [P])

                def pv(i):
                    qb, g = items[i]
                    kb0 = g * 4; nk = min(4, qb + 1 - kb0); P = Pt[i % 2]
                    acc = pacc[qb % 2]
                    B.op("pe", lambda: [nc.tensor.matmul(acc[:, c * 256:c * 256 + 129], P[:, c, j * 128:(j + 1) * 128], V[:, kb0 + j, h, 0:129],
                                                         start=(c == 0 and kb0 + j == 0), stop=(kb0 + j == qb), skip_group_check=True)
                                        for j in range(nk) for c in (0, 1)], [P, V], [acc])
                    if pend[1] is not None:
                        fin_b(pend[1])
                        pend[1] = None
                    if pend[0] is not None:
                        fin(pend[0])
                        pend[1] = pend[0]
                        pend[0] = None
                    if kb0 + nk - 1 == qb:
                        pend[0] = qb

                def fin(qb):
                    a0 = a1 = pacc[qb % 2]
                    V_ = nc.vector
                    B.op("dve", lambda: V_.reciprocal(rc[:, 0:1], a0[:, 128:129]), [a0], [rc])
                    B.op("dve", lambda: V_.reciprocal(rc[:, 1:2], a1[:, 384:385]), [a1], [rc])
                    B.op("dve", lambda: V_.tensor_tensor(rc[:, 1:2], rc[:, 1:2], nlam[:], ALU.mult), [rc, nlam], [rc])
                    B.op("dve", lambda: V_.tensor_scalar(o0[:], a0[:, 0:128], rc[:, 0:1], None, ALU.mult), [a0, rc], [o0])
                    B.op("dve", lambda: V_.scalar_tensor_tensor(out=a[:], in0=a1[:, 256:384], scalar=rc[:, 1:2], in1=o0[:],
                                                                op0=ALU.mult, op1=ALU.add), [a1, rc, o0], [a])
                    B.op("dve", lambda: V_.scalar_tensor_tensor(out=junk[:], in0=a[:], scalar=1.0, in1=a[:], op0=ALU.mult, op1=ALU.mult,
                                                                accum_out=ssq[:]), [a], [junk, ssq])
                    B.op("act", lambda: nc.scalar.activation(rs[:], ssq[:], AF.Ln, bias=EPS, scale=1.0 / 128), [ssq], [rs])
                    B.op("act", lambda: nc.scalar.activation(rstd[:], rs[:], AF.Exp, scale=-0.5), [rs], [rstd])
                    B.op("dve", lambda: V_.scalar_tensor_tensor(out=an[:], in0=a[:], scalar=rstd[:], in1=wrow[:],
                                                                op0=ALU.mult, op1=ALU.mult), [a, rstd, wrow], [an])

                def fin_b(qb):
                    O = ot[(qb // 4) % 2]
                    B.op("pe", lambda: nc.tensor.transpose(ptr[:, 0:128], an[:], ident[:]), [an, ident], [ptr])
                    B.op("act", lambda: nc.scalar.copy(O[:, (qb % 4) * 128:(qb % 4 + 1) * 128], ptr[:, 0:128]), [ptr], [O])
                    if qb % 4 == 3:
                        t0 = s * L + (qb // 4) * 512
                        B.dma(mixT[1024 + h * 128:1024 + (h + 1) * 128, t0:t0 + 512], O[:], reads=[O], q="act")

                pend = [None, None]
                score(0)
                for i in range(len(items)):
                    expmask(i)
                    if i + 1 < len(items):
                        score(i + 1)
                    pv(i)
                    yield
                if pend[1] is not None:
                    fin_b(pend[1])
                if pend[0] is not None:
                    fin(pend[0])
                    fin_b(pend[0])


def bc3(ap2, n):
    return ap2.unsqueeze(2).to_broadcast([ap2.shape[0], ap2.shape[1], n])


def bcm(ap2, n):
    return ap2.unsqueeze(1).to_broadcast([ap2.shape[0], n, ap2.shape[1]])


def st_gdn(B, l, es, nbanks=8):
    nc = B.nc
    L, NS = B.L, B.NS
    NC = L // 64
    NB = L // 512
    proj, mixT = B.d["proj"], B.d["mixT"]
    V, A, G = nc.vector, nc.scalar, nc.gpsimd
    if True:
        pbank = [B.ps(es, "pb") for _ in range(nbanks)]
        pbi = [0]

        npre = nbanks - 3
        pri = [0]

        def nbp():
            pbi[0] += 1
            return pbank[pbi[0] % npre]

        def nbr():
            pri[0] += 1
            return pbank[npre + pri[0] % 3]
        nb = nbp
        onesf = B.sb(es, "onesf", [128, 128]); ident = B.sb(es, "ident", [128, 128])
        B.op("pool", lambda: G.memset(onesf[:], 1.0), [], [onesf])
        B.op("pool", lambda: G.affine_select(out=ident[:], in_=onesf[:], pattern=[[-1, 128]], compare_op=ALU.is_equal,
                                             fill=0.0, base=0, channel_multiplier=1), [onesf], [ident])

        def mask64(name, pattern, cm, base, scale=None):
            t = B.sb(es, name, [64, 64])
            B.op("pool", lambda: G.affine_select(out=t[:], in_=onesf[0:64, 0:64], pattern=pattern, compare_op=ALU.is_ge,
                                                 fill=0.0, base=base, channel_multiplier=cm), [onesf], [t])
            if scale is not None:
                B.op("dve", lambda: V.tensor_scalar(t[:], t[:], scale, None, ALU.mult), [t], [t])
            return t
        U64 = mask64("U64", [[1, 64]], -1, 0)
        nU64 = mask64("nU64", [[1, 64]], -1, 0, -1.0)
        trilI = mask64("trilI", [[-1, 64]], 1, 0)
        strictL = mask64("strictL", [[-1, 64]], 1, -1)
        nstrictU = mask64("nstrictU", [[1, 64]], -1, -1, -1.0)
        nones64 = B.sb(es, "nones64", [64, 64])
        B.op("pool", lambda: G.memset(nones64[:], -1.0), [], [nones64])
        cw = B.sb(es, "cw", [128, 12, 4]); B.dma(cw[:], B.d["gdn_conv_w"][l], writes=[cw])
        gnw = B.sb(es, "gnw", [128, 1]); B.dma(gnw[:], B.d["gdn_norm_w"][l], writes=[gnw])
        nA = B.sb(es, "nA", [64, 4]); dtb = B.sb(es, "dtb", [64, 4])
        B.dma(nA[:], B.d["gdn_a_log"][l].partition_broadcast(64), writes=[nA])
        B.dma(dtb[:], B.d["gdn_dt_bias"][l].partition_broadcast(64), writes=[dtb])
        B.op("act", lambda: A.activation(nA[:], nA[:], AF.Exp), [nA], [nA])
        B.op("dve", lambda: V.tensor_scalar(nA[:], nA[:], -1.0, None, ALU.mult), [nA], [nA])
        ba = B.sb(es, "ba", [8, L])
        beta = B.sb(es, "beta", [64, NC, 4]); nbeta = B.sb(es, "nbeta", [64, NC, 4]); g_tm = B.sb(es, "g_tm", [64, NC, 4])
        gc_tm = B.sb(es, "gc_tm", [64, NC, 4]); egc = B.sb(es, "egc", [64, NC, 4]); ekd = B.sb(es, "ekd", [64, NC, 4])
        bg = B.sb(es, "bg", [64, NC, 4]); egl = B.sb(es, "egl", [128, NC, 4]); gl = B.sb(es, "gl", [128, NC, 4])
        raw = B.sb(es, "raw", [128, L + 3]); acc = B.sb(es, "acc", [128, L]); sqt = acc
        HB = [(B.sb(es, "qn", [128, L]), B.sb(es, "kn", [128, L]), B.sb(es, "vs", [128, L]),
               B.sb(es, "knb", [128, L], BF16), B.sb(es, "qnb", [128, L], BF16)) for _ in range(2)]
        rs5 = B.sb(es, "rs5", [128, 512]); rstd5 = B.sb(es, "rstd5", [128, 512])

        def t3(name, n=64):
            return B.sb(es, name, [64, 8, n])
        def t3b(name, n=64):
            return B.sb(es, name, [64, 8, n], BF16)
        k_tm, v_tm, o, on_ = t3("k_tm", 128), t3("v_tm", 128), t3("o", 128), t3("on", 128)
        kbg, vb = t3b("kbg", 128), t3b("vb", 128)
        G1, G2, dg, MbL, X0f, Z0f = t3("G1"), t3("G2"), t3("dg"), t3("MbL"), t3("X0f"), t3("Z0f")
        dec, decT = G1, G2
        X0, Z0, Xb, Zb, Qa, Qb = t3b("X0"), t3b("Z0"), t3b("Xb"), t3b("Zb"), t3b("Qa"), t3b("Qb")
        u2 = [t3("u", 128) for _ in range(2)]; kd2 = [t3b("kd", 128) for _ in range(2)]; attnT2 = [t3b("attnT") for _ in range(2)]
        wT2 = [B.sb(es, "wT", [128, 512], BF16) for _ in range(2)]; qgT2 = [B.sb(es, "qgT", [128, 512], BF16) for _ in range(2)]
        zt = B.sb(es, "zt", [128, 512])
        fin = B.sb(es, "fin", [128, 512], BF16)
        Sst = [B.sb(es, "S", [128, 128]) for _ in range(2)]
        Sbf = [B.sb(es, "Sb", [128, 128], BF16) for _ in range(2)]
        vnew = B.sb(es, "vnew", [64, 128], BF16); ss = B.sb(es, "ss", [64, 8]); rso = B.sb(es, "rso", [64, 8])
        for s in range(NS):
            c0 = s * L
            B.dma(ba[:], proj[R_BA:R_BA + 8, c0:c0 + L], writes=[ba])
            pt = nb()
            B.op("pe", lambda: [nc.tensor.transpose(pt[0:64, c * 8:(c + 1) * 8], ba[0:8, c * 64:(c + 1) * 64], ident[0:8, 0:8]) for c in range(NC)], [ba, ident], [pt])
            ptv = pt[0:64, 0:NC * 8].rearrange("p (c e) -> p c e", e=8)
            B.op("act", lambda: A.activation(beta[:], ptv[:, :, 0:4], AF.Sigmoid), [pt], [beta])
            B.op("dve", lambda: V.tensor_scalar(nbeta[:], beta[:], -1.0, None, ALU.mult), [beta], [nbeta])
            B.op("dve", lambda: V.tensor_tensor(g_tm[:], ptv[:, :, 4:8], bcm(dtb[:], NC), ALU.add), [pt, dtb], [g_tm])
            B.op("act", lambda: A.activation(g_tm[:], g_tm[:], AF.Exp), [g_tm], [g_tm])
            B.op("act", lambda: A.activation(g_tm[:], g_tm[:], AF.Ln, bias=1.0), [g_tm], [g_tm])
            B.op("dve", lambda: V.tensor_tensor(g_tm[:], g_tm[:], bcm(nA[:], NC), ALU.mult), [g_tm, nA], [g_tm])
            gflat = g_tm[:].rearrange("p c h -> p (c h)")
            p1 = nb()
            B.op("pe", lambda: nc.tensor.matmul(p1[0:64, 0:NC * 4], U64[:], gflat, start=True, stop=True), [U64, g_tm], [p1])
            B.op("dve", lambda: V.tensor_copy(gc_tm[:].rearrange("p c h -> p (c h)"), p1[0:64, 0:NC * 4]), [p1], [gc_tm])
            p2 = nb()
            B.op("pe", lambda: nc.tensor.matmul(p2[:, 0:NC * 4], onesf[0:64, :], gflat, start=True, stop=True), [onesf, g_tm], [p2])
            B.op("dve", lambda: V.tensor_copy(gl[:].rearrange("p c h -> p (c h)"), p2[:, 0:NC * 4]), [p2], [gl])
            B.op("act", lambda: A.activation(egl[:], gl[:], AF.Exp), [gl], [egl])
            B.op("act", lambda: A.activation(egc[:], gc_tm[:], AF.Exp), [gc_tm], [egc])
            B.op("dve", lambda: V.tensor_tensor(ekd[:], gl[0:64], gc_tm[:], ALU.subtract), [gl, gc_tm], [ekd])
            B.op("act", lambda: A.activation(ekd[:], ekd[:], AF.Exp), [ekd], [ekd])
            B.op("dve", lambda: V.tensor_tensor(bg[:], beta[:], egc[:], ALU.mult), [beta, egc], [bg])
            def prep(h):
                qn, kn, vs, knb, qnb = HB[h % 2]
                for (ci, dst, r0) in ((h, qn, R_GQ), (4 + h, kn, R_GK), (8 + h, vs, R_GV)):
                    B.op("pool", lambda: G.memset(raw[:, 0:3], 0.0), [], [raw])
                    B.dma(raw[:, 3:L + 3], proj[r0 + h * 128:r0 + (h + 1) * 128, c0:c0 + L], writes=[raw])
                    for tt in range(NB):
                        sl = slice(tt * 512, (tt + 1) * 512)
                        B.op("dve", lambda: V.tensor_scalar(acc[:, sl], raw[:, tt * 512:tt * 512 + 512], cw[:, ci, 0:1], None, ALU.mult), [raw, cw], [acc])
                        for j in range(1, 4):
                            B.op("dve", lambda: V.scalar_tensor_tensor(out=acc[:, sl], in0=raw[:, tt * 512 + j:tt * 512 + j + 512], scalar=cw[:, ci, j:j + 1],
                                                                       in1=acc[:, sl], op0=ALU.mult, op1=ALU.add), [raw, cw, acc], [acc])
                        yield
                    B.op("act", lambda: A.activation(dst[:], acc[:], AF.Silu), [acc], [dst])
                    yield
                for (dst, scl) in ((qn, 128.0 ** -0.5), (kn, 1.0)):
                    B.op("act", lambda: A.activation(sqt[:], dst[:], AF.Square), [dst], [sqt])
                    for tt in range(NB):
                        pn = nb()
                        B.op("pe", lambda: nc.tensor.matmul(pn[:], onesf[:], sqt[:, tt * 512:(tt + 1) * 512], start=True, stop=True), [onesf, sqt], [pn])
                        B.op("act", lambda: A.activation(rs5[:], pn[:], AF.Sqrt, bias=1e-6), [pn], [rs5])
                        B.op("dve", lambda: V.reciprocal(rstd5[:], rs5[:]), [rs5], [rstd5])
                        B.op("dve", lambda: V.scalar_tensor_tensor(out=dst[:, tt * 512:(tt + 1) * 512], in0=dst[:, tt * 512:(tt + 1) * 512], scalar=scl,
                                                                   in1=rstd5[:], op0=ALU.mult, op1=ALU.mult), [dst, rstd5], [dst])
                        yield
                B.op("pool", lambda: G.tensor_copy(knb[:], kn[:]), [kn], [knb])
                B.op("pool", lambda: G.tensor_copy(qnb[:], qn[:]), [qn], [qnb])
                yield

            def drive0(*gs):
                gs = [g for g in gs if g is not None]
                while gs:
                    for g in list(gs):
                        try:
                            next(g)
                        except StopIteration:
                            gs.remove(g)
                    yield
            nxt = prep(0)
            yield from drive0(nxt)
            for h in range(4):
                qn, kn, vs, knb, qnb = HB[h % 2]
                nxt = prep(h + 1) if h + 1 < 4 else None
                B.op("pool", lambda: G.memset(Sst[0][:], 0.0), [], [Sst[0]])
                B.op("pool", lambda: G.memset(Sbf[0][:], 0.0), [], [Sbf[0]])
                sist = [0]

                def pre(b):
                    bs = b * 512
                    cs = slice(b * 8, (b + 1) * 8)
                    u, wT, qgT, attnT, kd = u2[b % 2], wT2[b % 2], qgT2[b % 2], attnT2[b % 2], kd2[b % 2]

                    def hb(t):
                        return t[:, cs, h]
                    for (src, dst) in ((kn, k_tm), (vs, v_tm)):
                        for half in range(2):
                            pk = nbp()
                            B.op("pe", lambda: [nc.tensor.transpose(pk[0:64, q * 128:(q + 1) * 128], src[:, bs + (half * 4 + q) * 64: bs + (half * 4 + q + 1) * 64], ident[:])
                                                for q in range(4)], [src, ident], [pk])
                            B.op("act", lambda: A.copy(dst[:, half * 4:(half + 1) * 4, :].rearrange("p c e -> p (c e)"), pk[0:64, :]), [pk], [dst])
                    yield
                    B.op("dve", lambda: V.tensor_copy(G1[:], bc3(hb(g_tm), 64)), [g_tm], [G1])
                    B.op("dve", lambda: V.tensor_tensor(G2[:], bcm(U64[:], 8), bc3(hb(g_tm), 64), ALU.mult), [U64, g_tm], [G2])
                    G1f, G2f = G1[:].rearrange("p c e -> p (c e)"), G2[:].rearrange("p c e -> p (c e)")
                    pD, pDT = nbp(), nbp()
                    B.op("pe", lambda: [nc.tensor.matmul(pD[0:64, :], U64[:], G1f, start=True, stop=False),
                                        nc.tensor.matmul(pD[0:64, :], nones64[:], G2f, start=False, stop=True)], [U64, nones64, G1, G2], [pD])
                    B.op("pe", lambda: [nc.tensor.matmul(pDT[0:64, :], onesf[0:64, 0:64], G2f, start=True, stop=False),
                                        nc.tensor.matmul(pDT[0:64, :], nU64[:], G1f, start=False, stop=True)], [onesf, nU64, G1, G2], [pDT])
                    for (pp, dd, mk) in ((pD, dec, trilI), (pDT, decT, U64)):
                        ddf = dd[:].rearrange("p c e -> p (c e)")
                        B.op("dve", lambda: V.tensor_scalar(ddf, pp[0:64, :], 0.0, None, ALU.min), [pp], [dd])
                        B.op("act", lambda: A.activation(ddf, ddf, AF.Exp), [dd], [dd])
                        B.op("pool", lambda: G.tensor_tensor(dd[:], dd[:], bcm(mk[:], 8), ALU.mult), [dd, mk], [dd])
                    yield
                    pKK = nbp()
                    B.op("pe", lambda: [nc.tensor.matmul(pKK[0:64, c * 64:(c + 1) * 64], knb[:, bs + c * 64:bs + (c + 1) * 64], knb[:, bs + c * 64:bs + (c + 1) * 64], start=True, stop=True)
                                        for c in range(8)], [knb], [pKK])
                    B.op("pool", lambda: G.tensor_tensor(MbL[:], bcm(strictL[:], 8), bc3(hb(nbeta), 64), ALU.mult), [strictL, nbeta], [MbL])
                    B.op("dve", lambda: V.tensor_tensor(X0f[:].rearrange("p c e -> p (c e)"), pKK[0:64, :], dec[:].rearrange("p c e -> p (c e)"), ALU.mult), [pKK, dec], [X0f])
                    B.op("dve", lambda: V.tensor_tensor(X0[:], X0f[:], MbL[:], ALU.mult), [X0f, MbL], [X0])
                    B.op("dve", lambda: V.tensor_tensor(Z0f[:].rearrange("p c e -> p (c e)"), pKK[0:64, :], decT[:].rearrange("p c e -> p (c e)"), ALU.mult), [pKK, decT], [Z0f])
                    B.op("dve", lambda: V.tensor_tensor(dg[:], bcm(ident[0:64, 0:64], 8), bc3(hb(beta), 64), ALU.mult), [ident, beta], [dg])
                    pBR = nbp()
                    B.op("pe", lambda: nc.tensor.matmul(pBR[0:64, :], onesf[0:64, 0:64], dg[:].rearrange("p c e -> p (c e)"), start=True, stop=True), [onesf, dg], [pBR])
                    B.op("dve", lambda: V.tensor_tensor(Z0f[:].rearrange("p c e -> p (c e)"), Z0f[:].rearrange("p c e -> p (c e)"), pBR[0:64, :], ALU.mult), [Z0f, pBR], [Z0f])
                    B.op("pool", lambda: G.tensor_tensor(Z0f[:], Z0f[:], bcm(nstrictU[:], 8), ALU.mult), [Z0f, nstrictU], [Z0f])
                    B.op("pool", lambda: G.tensor_copy(Z0[:], Z0f[:]), [Z0f], [Z0])
                    pQK = nbp()
                    B.op("pe", lambda: [nc.tensor.matmul(pQK[0:64, c * 64:(c + 1) * 64], knb[:, bs + c * 64:bs + (c + 1) * 64], qnb[:, bs + c * 64:bs + (c + 1) * 64], start=True, stop=True)
                                        for c in range(8)], [knb, qnb], [pQK])
                    B.op("dve", lambda: V.tensor_tensor(attnT[:].rearrange("p c e -> p (c e)"), pQK[0:64, :], decT[:].rearrange("p c e -> p (c e)"), ALU.mult), [pQK, decT], [attnT])
                    B.op("pool", lambda: G.tensor_tensor(Qa[:], Z0f[:], bcm(ident[0:64, 0:64], 8), ALU.add), [Z0f, ident], [Qa])
                    Xc, Zc, Qc = X0, Z0, Qa
                    Xn, Zn, Qn = Xb, Zb, Qb
                    for lev in range(5):
                        pX, pZ = nbp(), nbp()
                        B.op("pe", lambda: [nc.tensor.matmul(pX[0:64, c * 64:(c + 1) * 64], Zc[:, c, :], Xc[:, c, :], start=True, stop=True) for c in range(8)], [Zc, Xc], [pX])
                        if lev < 4:
                            B.op("pe", lambda: [nc.tensor.matmul(pZ[0:64, c * 64:(c + 1) * 64], Xc[:, c, :], Zc[:, c, :], start=True, stop=True) for c in range(8)], [Zc, Xc], [pZ])
                        B.op("dve", lambda: V.tensor_copy(Xn[:].rearrange("p c e -> p (c e)"), pX[0:64, :]), [pX], [Xn])
                        if lev < 4:
                            B.op("act", lambda: A.copy(Zn[:].rearrange("p c e -> p (c e)"), pZ[0:64, :]), [pZ], [Zn])
                        pQ = nbp()
                        B.op("pe", lambda: [nc.tensor.matmul(pQ[0:64, c * 64:(c + 1) * 64], Xn[:, c, :], Qc[:, c, :], start=True, stop=True) for c in range(8)], [Xn, Qc], [pQ])
                        B.op("dve", lambda: V.tensor_tensor(Qn[:].rearrange("p c e -> p (c e)"), Qc[:].rearrange("p c e -> p (c e)"), pQ[0:64, :], ALU.add), [Qc, pQ], [Qn])
                        yield
                        Xc, Xn = Xn, Xc
                        Zc, Zn = Zn, Zc
                        Qc, Qn = Qn, Qc
                    TT = Qc
                    B.op("pool", lambda: G.tensor_tensor(kbg[:], k_tm[:], bc3(hb(bg), 128), ALU.mult), [k_tm, bg], [kbg])
                    B.op("pool", lambda: G.tensor_tensor(kd[:], k_tm[:], bc3(hb(ekd), 128), ALU.mult), [k_tm, ekd], [kd])
                    B.op("pool", lambda: G.tensor_tensor(vb[:], v_tm[:], bc3(hb(beta), 128), ALU.mult), [v_tm, beta], [vb])
                    for half in range(2):
                        pu = nbp()
                        B.op("pe", lambda: [nc.tensor.matmul(pu[0:64, q * 128:(q + 1) * 128], TT[:, half * 4 + q, :], vb[:, half * 4 + q, :], start=True, stop=True) for q in range(4)], [TT, vb], [pu])
                        B.op("act", lambda: A.copy(u[:, half * 4:(half + 1) * 4, :].rearrange("p c e -> p (c e)"), pu[0:64, :]), [pu], [u])
                    pw = nbp()
                    B.op("pe", lambda: [nc.tensor.matmul(pw[:, c * 64:(c + 1) * 64], kbg[:, c, :], TT[:, c, :], start=True, stop=True) for c in range(8)], [kbg, TT], [pw])
                    B.op("act", lambda: A.copy(wT[:], pw[:]), [pw], [wT])
                    B.op("dve", lambda: V.tensor_tensor(dg[:], bcm(ident[0:64, 0:64], 8), bc3(hb(egc), 64), ALU.mult), [ident, egc], [dg])
                    pe_ = nbp()
                    B.op("pe", lambda: nc.tensor.matmul(pe_[:], onesf[0:64, :], dg[:].rearrange("p c e -> p (c e)"), start=True, stop=True), [onesf, dg], [pe_])
                    B.op("dve", lambda: V.tensor_tensor(qgT[:], qn[:, bs:bs + 512], pe_[:], ALU.mult), [qn, pe_], [qgT])
                    yield

                def rec(b):
                    bs = b * 512
                    cs = slice(b * 8, (b + 1) * 8)
                    u, wT, qgT, attnT, kd = u2[b % 2], wT2[b % 2], qgT2[b % 2], attnT2[b % 2], kd2[b % 2]

                    def hb(t):
                        return t[:, cs, h]
                    for c in range(8):
                        Sc, Sn = Sst[sist[0] % 2], Sst[(sist[0] + 1) % 2]
                        Sbc, Sbn = Sbf[sist[0] % 2], Sbf[(sist[0] + 1) % 2]
                        sist[0] += 1
                        pW, pO, pS = nbr(), nbr(), nbr()
                        B.op("pe", lambda: nc.tensor.matmul(pW[0:64, 0:128], wT[:, c * 64:(c + 1) * 64], Sbc[:], start=True, stop=True), [wT, Sbc], [pW])
                        B.op("dve", lambda: V.tensor_tensor(vnew[:], u[:, c, :], pW[0:64, 0:128], ALU.subtract), [u, pW], [vnew])
                        B.op("pe", lambda: [nc.tensor.matmul(pO[0:64, 0:128], qgT[:, c * 64:(c + 1) * 64], Sbc[:], start=True, stop=False),
                                            nc.tensor.matmul(pO[0:64, 0:128], attnT[:, c, :], vnew[:], start=False, stop=True)], [qgT, Sbc, attnT, vnew], [pO])
                        B.op("pe", lambda: nc.tensor.matmul(pS[:, 0:128], kd[:, c, :], vnew[:], start=True, stop=True), [kd, vnew], [pS])
                        B.op("act", lambda: A.copy(o[:, c, :], pO[0:64, 0:128]), [pO], [o])
                        B.op("dve", lambda: V.scalar_tensor_tensor(out=Sbn[:], in0=Sc[:], scalar=egl[:, b * 8 + c, h:h + 1], in1=pS[:, 0:128],
                                                                   op0=ALU.mult, op1=ALU.add), [Sc, egl, pS], [Sbn])
                        B.op("dve", lambda: V.scalar_tensor_tensor(out=Sn[:], in0=Sc[:], scalar=egl[:, b * 8 + c, h:h + 1], in1=pS[:, 0:128],
                                                                   op0=ALU.mult, op1=ALU.add), [Sc, egl, pS], [Sn])
                        yield

                def epi(b):
                    bs = b * 512
                    cs = slice(b * 8, (b + 1) * 8)
                    u, wT, qgT, attnT, kd = u2[b % 2], wT2[b % 2], qgT2[b % 2], attnT2[b % 2], kd2[b % 2]

                    def hb(t):
                        return t[:, cs, h]
                    B.op("pool", lambda: G.tensor_tensor(on_[:], o[:], o[:], ALU.mult), [o], [on_])
                    B.op("dve", lambda: V.tensor_reduce(ss[:], on_[:], AX.X, ALU.add), [on_], [ss])
                    B.op("act", lambda: A.activation(rso[:], ss[:], AF.Sqrt, bias=EPS, scale=1.0 / 128), [ss], [rso])
                    B.op("dve", lambda: V.reciprocal(rso[:], rso[:]), [rso], [rso])
                    B.op("dve", lambda: V.tensor_tensor(on_[:], o[:], bc3(rso[:], 128), ALU.mult), [o, rso], [on_])
                    pT = nbp()
                    B.op("pe", lambda: [nc.tensor.transpose(pT[:, c * 64:(c + 1) * 64], on_[:, c, :], ident[0:64, 0:64]) for c in range(8)], [on_, ident], [pT])
                    B.dma(zt[:], proj[R_GZ + h * 128:R_GZ + (h + 1) * 128, c0 + bs:c0 + bs + 512], writes=[zt])
                    B.op("act", lambda: A.activation(zt[:], zt[:], AF.Silu), [zt], [zt])
                    B.op("dve", lambda: V.scalar_tensor_tensor(out=fin[:], in0=pT[:], scalar=gnw[:, 0:1], in1=zt[:], op0=ALU.mult, op1=ALU.mult), [pT, gnw, zt], [fin])
                    B.dma(mixT[h * 128:(h + 1) * 128, c0 + bs:c0 + bs + 512], fin[:], reads=[fin])

                def drive(*gs):
                    gs = [g for g in gs if g is not None]
                    while gs:
                        for g in list(gs):
                            try:
                                next(g)
                            except StopIteration:
                                gs.remove(g)
                        yield
                yield from drive(pre(0))
                for b in range(NB):
                    yield from drive(rec(b), pre(b + 1) if b + 1 < NB else None, nxt)
                    epi(b)
                    yield


def bc3(ap2, n):
    return ap2.unsqueeze(2).to_broadcast([ap2.shape[0], ap2.shape[1], n])


def s5_persist(B, es):
    ctx = {}
    ctx["r"] = B.sb(es, "r", [128, 32])
    for nm in ("L1", "L2", "CW1", "CW2"):
        ctx[nm] = B.sb(es, nm, [128, 32, 128], BF16)
    return ctx


def st_s5_setup(B, l, es, ctx):
    nc = B.nc
    L, NS, TOK = B.L, B.NS, B.TOK
    NLV = int(math.log2(L))
    proj, mixT, Ctab, Stab = B.d["proj"], B.d["mixT"], B.d["s5_ctab"], B.d["s5_stab"]
    V = nc.vector
    r, L1, L2, CW1, CW2 = ctx["r"], ctx["L1"], ctx["L2"], ctx["CW1"], ctx["CW2"]
    if True:
        def t32(name):
            return B.sb(es, name, [128, 32])
        lre, lim, dt, xr, th = t32("lre"), t32("lim"), t32("dt"), t32("xr"), t32("th")
        c, s_, c2, s2, tmp, tmp2 = t32("c"), t32("s"), t32("c2"), t32("s2"), t32("tmp"), t32("tmp2")
        nr, ni, den, cre, cim, cimA, creB = t32("nr"), t32("ni"), t32("den"), t32("cre"), t32("cim"), t32("cimA"), t32("creB")
        B.dma(lre[:], B.d["s5_lre"][l], writes=[lre]); B.dma(lim[:], B.d["s5_lim"][l], writes=[lim])
        B.dma(dt[:], B.d["s5_log_dt"][l].partition_broadcast(128), writes=[dt])
        sgA = B.sb(es, "sgA", [128, 1]); sgB = B.sb(es, "sgB", [128, 1])
        B.op("pool", lambda: [nc.gpsimd.memset(sgA[0:64, :], -1.0), nc.gpsimd.memset(sgA[64:128, :], 1.0)], [], [sgA])
        B.op("pool", lambda: [nc.gpsimd.memset(sgB[0:64, :], 1.0), nc.gpsimd.memset(sgB[64:128, :], -1.0)], [], [sgB])
        B.op("dve", lambda: V.tensor_scalar(lre[:], lre[:], -1e-4, None, ALU.min), [lre], [lre])
        B.op("act", lambda: nc.scalar.activation(dt[:], dt[:], AF.Exp), [dt], [dt])
        B.op("dve", lambda: V.tensor_tensor(xr[:], lre[:], dt[:], ALU.mult), [lre, dt], [xr])
        B.op("dve", lambda: V.tensor_tensor(th[:], lim[:], dt[:], ALU.mult), [lim, dt], [th])
        B.op("act", lambda: nc.scalar.activation(r[:], xr[:], AF.Exp), [xr], [r])
        B.op("act", lambda: nc.scalar.activation(s_[:], th[:], AF.Sin, scale=0.125), [th], [s_])
        B.op("act", lambda: nc.scalar.activation(c[:], th[:], AF.Sin, bias=math.pi / 2, scale=-0.125), [th], [c])

        def dbl(ci, si, co, so):
            B.op("dve", lambda: V.tensor_tensor(tmp[:], ci[:], ci[:], ALU.mult), [ci], [tmp])
            B.op("dve", lambda: V.tensor_tensor(tmp2[:], si[:], si[:], ALU.mult), [si], [tmp2])
            B.op("dve", lambda: V.tensor_tensor(so[:], ci[:], si[:], ALU.mult), [ci, si], [so])
            B.op("dve", lambda: V.tensor_tensor(co[:], tmp[:], tmp2[:], ALU.subtract), [tmp, tmp2], [co])
            B.op("dve", lambda: V.tensor_scalar(so[:], so[:], 2.0, None, ALU.mult), [so], [so])
        dbl(c, s_, c2, s2); dbl(c2, s2, c, s_); dbl(c, s_, c2, s2)
        CK = B.sb(es, "CK", [128, NLV, 32]); SK = B.sb(es, "SK", [128, NLV, 32]); NSK = B.sb(es, "NSK", [128, NLV, 32])
        B.op("dve", lambda: V.tensor_copy(CK[:, 0, :], c2[:]), [c2], [CK])
        B.op("dve", lambda: V.tensor_copy(SK[:, 0, :], s2[:]), [s2], [SK])
        for k in range(1, NLV):
            B.op("dve", lambda: V.tensor_tensor(tmp[:], CK[:, k - 1, :], CK[:, k - 1, :], ALU.mult), [CK], [tmp])
            B.op("dve", lambda: V.tensor_tensor(tmp2[:], SK[:, k - 1, :], SK[:, k - 1, :], ALU.mult), [SK], [tmp2])
            B.op("dve", lambda: V.tensor_tensor(CK[:, k, :], tmp[:], tmp2[:], ALU.subtract), [tmp, tmp2], [CK])
            B.op("dve", lambda: V.tensor_tensor(tmp[:], CK[:, k - 1, :], SK[:, k - 1, :], ALU.mult), [CK, SK], [tmp])
            B.op("dve", lambda: V.tensor_scalar(SK[:, k, :], tmp[:], 2.0, None, ALU.mult), [tmp], [SK])
        B.op("dve", lambda: V.tensor_scalar(NSK[:], SK[:], -1.0, None, ALU.mult), [SK], [NSK])
        B.op("dve", lambda: V.tensor_tensor(nr[:], r[:], c2[:], ALU.mult), [r, c2], [nr])
        B.op("dve", lambda: V.tensor_scalar(nr[:], nr[:], -1.0, None, ALU.add), [nr], [nr])
        B.op("dve", lambda: V.tensor_tensor(ni[:], r[:], s2[:], ALU.mult), [r, s2], [ni])
        B.op("dve", lambda: V.tensor_tensor(den[:], lre[:], lre[:], ALU.mult), [lre], [den])
        B.op("dve", lambda: V.tensor_tensor(tmp[:], lim[:], lim[:], ALU.mult), [lim], [tmp])
        B.op("dve", lambda: V.tensor_tensor(den[:], den[:], tmp[:], ALU.add), [den, tmp], [den])
        B.op("dve", lambda: V.reciprocal(den[:], den[:]), [den], [den])
        B.op("dve", lambda: V.tensor_tensor(tmp[:], nr[:], lre[:], ALU.mult), [nr, lre], [tmp])
        B.op("dve", lambda: V.tensor_tensor(tmp2[:], ni[:], lim[:], ALU.mult), [ni, lim], [tmp2])
        B.op("dve", lambda: V.tensor_tensor(cre[:], tmp[:], tmp2[:], ALU.add), [tmp, tmp2], [cre])
        B.op("dve", lambda: V.tensor_tensor(cre[:], cre[:], den[:], ALU.mult), [cre, den], [cre])
        B.op("dve", lambda: V.tensor_tensor(tmp[:], ni[:], lre[:], ALU.mult), [ni, lre], [tmp])
        B.op("dve", lambda: V.tensor_tensor(tmp2[:], nr[:], lim[:], ALU.mult), [nr, lim], [tmp2])
        B.op("dve", lambda: V.tensor_tensor(cim[:], tmp[:], tmp2[:], ALU.subtract), [tmp, tmp2], [cim])
        B.op("dve", lambda: V.tensor_tensor(cim[:], cim[:], den[:], ALU.mult), [cim, den], [cim])
        B.op("dve", lambda: V.tensor_scalar(cimA[:], cim[:], sgA[:, 0:1], None, ALU.mult), [cim, sgA], [cimA])
        B.op("dve", lambda: V.tensor_scalar(creB[:], cre[:], sgB[:, 0:1], None, ALU.mult), [cre, sgB], [creB])
        XA = B.sb(es, "XA", [128, 32, 16]); XB = B.sb(es, "XB", [128, 32, 16])
        B.dma(XA[:], B.d["s5_XA"][l], writes=[XA]); B.dma(XB[:], B.d["s5_XB"][l], writes=[XB])
        B1T = B.sb(es, "B1T", [128, 32, 16]); B2T = B.sb(es, "B2T", [128, 32, 16]); bt3 = B.sb(es, "bt3", [128, 32, 16])
        B.op("dve", lambda: V.tensor_tensor(B1T[:], XA[:], bc3(cre[:], 16), ALU.mult), [XA, cre], [B1T])
        B.op("dve", lambda: V.tensor_tensor(bt3[:], XB[:], bc3(cimA[:], 16), ALU.mult), [XB, cimA], [bt3])
        B.op("dve", lambda: V.tensor_tensor(B1T[:], B1T[:], bt3[:], ALU.add), [B1T, bt3], [B1T])
        B.op("dve", lambda: V.tensor_tensor(B2T[:], XB[:], bc3(creB[:], 16), ALU.mult), [XB, creB], [B2T])
        B.op("dve", lambda: V.tensor_tensor(bt3[:], XA[:], bc3(cim[:], 16), ALU.mult), [XA, cim], [bt3])
        B.op("dve", lambda: V.tensor_tensor(B2T[:], B2T[:], bt3[:], ALU.add), [B2T, bt3], [B2T])
        onesf = B.sb(es, "onesf", [128, 128]); ident = B.sb(es, "ident", [128, 128])
        m1 = B.sb(es, "m1", [128, 8]); mask8 = B.sb(es, "mask8", [128, 8])
        B.op("pool", lambda: nc.gpsimd.memset(onesf[:], 1.0), [], [onesf])
        B.op("pool", lambda: nc.gpsimd.affine_select(out=ident[:], in_=onesf[:], pattern=[[-1, 128]], compare_op=ALU.is_equal,
                                                     fill=0.0, base=0, channel_multiplier=1), [onesf], [ident])
        B.op("pool", lambda: nc.gpsimd.affine_select(out=m1[:], in_=onesf[:, 0:8], pattern=[[-16, 8]], compare_op=ALU.is_ge,
                                                     fill=0.0, base=0, channel_multiplier=1), [onesf], [m1])
        B.op("pool", lambda: nc.gpsimd.affine_select(out=mask8[:], in_=m1[:], pattern=[[16, 8]], compare_op=ALU.is_ge,
                                                     fill=0.0, base=15, channel_multiplier=-1), [m1], [mask8])
        ptr = B.ps(es, "PAB")
        yield
        for (src, dst) in ((B1T, L1), (B2T, L2)):
            for gc in range(4):
                B.op("pe", lambda: nc.tensor.transpose(ptr[:, 0:128], src[:, gc * 8:(gc + 1) * 8, :], ident[:]), [src, ident], [ptr])
                for j in range(8):
                    B.op("dve", lambda: V.tensor_scalar(dst[:, gc * 8 + j, :], ptr[:, 0:128], mask8[:, j:j + 1], None, ALU.mult), [ptr, mask8], [dst])
        CA, CB = XA, XB
        B.dma(CA[:], B.d["s5_CA"][l], writes=[CA]); B.dma(CB[:], B.d["s5_CB"][l], writes=[CB])
        yield
        B.op("pool", lambda: nc.gpsimd.memset(CW1[:], 0.0), [], [CW1])
        B.op("pool", lambda: nc.gpsimd.memset(CW2[:], 0.0), [], [CW2])
        for gc in range(4):
            for j in range(8):
                g = gc * 8 + j
                B.op("dve", lambda: V.tensor_scalar(CW1[:, g, 16 * j:16 * j + 16], CA[:, g, :], sgB[:, 0:1], None, ALU.mult), [CA, sgB], [CW1])
                B.op("dve", lambda: V.tensor_scalar(CW2[:, g, 16 * j:16 * j + 16], CB[:, g, :], -1.0, None, ALU.mult), [CB], [CW2])
        if True:
            es2 = es
            GB = 1
            tc_ = B.sb(es2, "tc", [128, GB, L]); ts_ = B.sb(es2, "ts", [128, GB, L])
            tq = B.sb(es2, "tq", [128, GB, L // 2]); tq2 = B.sb(es2, "tq2", [128, GB, L // 2])
            for gb in range(32 // GB):
                gs = slice(gb * GB, (gb + 1) * GB)
                B.op("dve", lambda: V.memset(tc_[:, :, 0:1], 1.0), [], [tc_])
                B.op("dve", lambda: V.memset(ts_[:, :, 0:1], 0.0), [], [ts_])
                for k in range(NLV):
                    n = 1 << k
                    ckp, skp, nskp = CK[:, k, gb:gb + 1], SK[:, k, gb:gb + 1], NSK[:, k, gb:gb + 1]
                    B.op("dve", lambda: V.tensor_scalar(tq[:, 0, 0:n], ts_[:, 0, 0:n], nskp, None, ALU.mult), [ts_, NSK], [tq])
                    B.op("dve", lambda: V.tensor_scalar(tq2[:, 0, 0:n], tc_[:, 0, 0:n], skp, None, ALU.mult), [tc_, SK], [tq2])
                    B.op("dve", lambda: V.scalar_tensor_tensor(out=tc_[:, 0, n:2 * n], in0=tc_[:, 0, 0:n], scalar=ckp, in1=tq[:, 0, 0:n],
                                                               op0=ALU.mult, op1=ALU.add), [tc_, CK, tq], [tc_])
                    B.op("dve", lambda: V.scalar_tensor_tensor(out=ts_[:, 0, n:2 * n], in0=ts_[:, 0, 0:n], scalar=ckp, in1=tq2[:, 0, 0:n],
                                                               op0=ALU.mult, op1=ALU.add), [ts_, CK, tq2], [ts_])
                    if k >= 6:
                        yield
                B.dma(Ctab[gs].rearrange("g p t -> p g t"), tc_[:], reads=[tc_])
                B.dma(Stab[gs].rearrange("g p t -> p g t"), ts_[:], reads=[ts_])
                yield


def st_s5_main(B, l, es, ctx):
    nc = B.nc
    L, NS, TOK = B.L, B.NS, B.TOK
    proj, mixT, Ctab, Stab = B.d["proj"], B.d["mixT"], B.d["s5_ctab"], B.d["s5_stab"]
    V = nc.vector
    r, L1, L2, CW1, CW2 = ctx["r"], ctx["L1"], ctx["L2"], ctx["CW1"], ctx["CW2"]
    if True:
        uTs = [B.sb(es, "uT", [128, 4, 512], BF16) for _ in range(2)]
        ust = B.sb(es, "ust", [128, 4, 512])
        dsk = B.sb(es, "dsk", [128, 4]); bgl = B.sb(es, "bgl", [128, 4]); nws = B.sb(es, "nws", [128, 4])
        B.dma(dsk[:], B.d["s5_d"][l], writes=[dsk]); B.dma(bgl[:], B.d["s5_b_glu"][l], writes=[bgl]); B.dma(nws[:], B.d["s5_norm_w"][l], writes=[nws])
        wg = [B.sb(es, "wg", [128, 512], BF16) for _ in range(4)]
        wgv = B.d["s5_w_glu"][l].rearrange("(kc p) n -> p kc n", p=128)
        for kc in range(4):
            B.dma(wg[kc][:], wgv[:, kc, :], writes=[wg[kc]], q="pool")
        ones = B.sb(es, "ones", [128, 128], BF16)
        B.op("pool", lambda: nc.gpsimd.memset(ones[:], 1.0), [], [ones])
        wlast = B.sb(es, "wlast", [128, 32])
        NTB = 4
        Ct = [B.sb(es, "Ct", [128, 512]) for _ in range(NTB)]; St = [B.sb(es, "St", [128, 512]) for _ in range(NTB)]
        PA = [B.ps(es, "PA") for _ in range(2)]; PB = [B.ps(es, "PB") for _ in range(2)]
        yps = B.ps(es, "yps"); pz = B.ps(es, "pz"); pss = pz
        t1 = [B.sb(es, "t1", [128, 512]) for _ in range(2)]; t2 = [B.sb(es, "t2", [128, 512]) for _ in range(2)]
        bt = [B.sb(es, "bt", [128, 512]) for _ in range(2)]; wv = [B.sb(es, "w", [128, 512]) for _ in range(2)]
        Wc = [B.sb(es, "Wc", [128, 512], BF16) for _ in range(2)]; Ws = [B.sb(es, "Ws", [128, 512], BF16) for _ in range(2)]
        uf = [B.sb(es, "uf", [128, 512]) for _ in range(2)]
        yg = B.sb(es, "yg", [128, 4, 512]); ygb = B.sb(es, "ygb", [128, 4, 512], BF16)
        sig = B.sb(es, "sig", [128, 512]); o = B.sb(es, "o", [128, 4, 512]); sq = B.sb(es, "sq", [128, 4, 512], BF16)
        rs = B.sb(es, "rs", [128, 512]); rstd = B.sb(es, "rstd", [128, 512]); ob = B.sb(es, "ob", [128, 4, 512], BF16)
        items = [(s, tile, g) for s in range(NS) for tile in range(L // 512) for g in range(32)]

        def front(i):
            s, tile, g = items[i]
            t0 = s * L + tile * 512
            uT = uTs[(s * (L // 512) + tile) % 2]
            if g == 0:
                B.dma(ust[:], proj[R_SU:R_SU + 512, t0:t0 + 512].rearrange("(c p) t -> p c t", p=128), writes=[ust])
                B.op("act", lambda: nc.scalar.copy(uT[:], ust[:]), [ust], [uT])
            B.dma(Ct[i % NTB][:], Ctab[g, :, tile * 512:(tile + 1) * 512], writes=[Ct[i % NTB]])
            B.dma(St[i % NTB][:], Stab[g, :, tile * 512:(tile + 1) * 512], writes=[St[i % NTB]])
            B.op("pe", lambda: nc.tensor.matmul(PA[i % 2][:], L1[:, g, :], uT[:, g // 8, :], start=True, stop=True), [L1, uT], [PA[i % 2]])
            B.op("pe", lambda: nc.tensor.matmul(PB[i % 2][:], L2[:, g, :], uT[:, g // 8, :], start=True, stop=True), [L2, uT], [PB[i % 2]])

        def mid0(i):
            s, tile, g = items[i]
            b = i % 2
            B.op("dve", lambda: V.tensor_tensor(t1[b][:], PA[b][:], Ct[i % NTB][:], ALU.mult), [PA[b], Ct[i % NTB]], [t1[b]])
            B.op("dve", lambda: V.tensor_tensor(t2[b][:], PB[b][:], St[i % NTB][:], ALU.mult), [PB[b], St[i % NTB]], [t2[b]])

        def mid(i):
            s, tile, g = items[i]
            b = i % 2
            B.op("dve", lambda: V.tensor_tensor(bt[b][:], t1[b][:], t2[b][:], ALU.add), [t1[b], t2[b]], [bt[b]])
            init = 0.0 if tile == 0 else wlast[:, g:g + 1]
            B.op("dve", lambda: V.tensor_tensor_scan(wv[b][:], r[:, g:g + 1].to_broadcast([128, 512]), bt[b][:], init, ALU.mult, ALU.add),
                 [r, bt[b], wlast], [wv[b]])
            B.op("act", lambda: nc.scalar.copy(wlast[:, g:g + 1], wv[b][:, 511:512]), [wv[b]], [wlast])
            B.op("pool", lambda: nc.gpsimd.tensor_tensor(Wc[b][:], wv[b][:], Ct[i % NTB][:], ALU.mult), [wv[b], Ct[i % NTB]], [Wc[b]])
            B.op("pool", lambda: nc.gpsimd.tensor_tensor(Ws[b][:], wv[b][:], St[i % NTB][:], ALU.mult), [wv[b], St[i % NTB]], [Ws[b]])

        def back(i):
            s, tile, g = items[i]
            b = i % 2
            j = g % 8
            B.op("pe", lambda: [nc.tensor.matmul(yps[:], CW1[:, g, :], Wc[b][:], start=(j == 0), stop=False),
                                nc.tensor.matmul(yps[:], CW2[:, g, :], Ws[b][:], start=False, stop=(j == 7))], [CW1, CW2, Wc[b], Ws[b]], [yps])
            if j == 7:
                epi(s, tile, g // 8)

        def epi(s, tile, gc):
            t0 = s * L + tile * 512
            U = uf[gc % 2]
            B.dma(U[:], proj[R_SU + gc * 128:R_SU + (gc + 1) * 128, t0:t0 + 512], writes=[U])
            B.op("dve", lambda: V.scalar_tensor_tensor(out=U[:], in0=U[:], scalar=dsk[:, gc:gc + 1], in1=yps[:], op0=ALU.mult, op1=ALU.add), [U, dsk, yps], [U])
            B.op("act", lambda: nc.scalar.activation(yg[:, gc, :], U[:], AF.Gelu), [U], [yg])
            B.op("pool", lambda: nc.gpsimd.tensor_copy(ygb[:, gc, :], yg[:, gc, :]), [yg], [ygb])
            if gc == 3:
                for oc in range(4):
                    B.op("pe", lambda: mm_acc(nc, pz[:], [(wg[kc][:, oc * 128:(oc + 1) * 128], ygb[:, kc, :]) for kc in range(4)]), [ygb] + wg, [pz])
                    B.op("act", lambda: nc.scalar.activation(sig[:], pz[:], AF.Sigmoid, bias=bgl[:, oc:oc + 1]), [pz, bgl], [sig])
                    B.op("dve", lambda: V.tensor_tensor(o[:, oc, :], yg[:, oc, :], sig[:], ALU.mult), [yg, sig], [o])
                B.op("act", lambda: nc.scalar.activation(sq[:], o[:], AF.Square), [o], [sq])
                B.op("pe", lambda: mm_acc(nc, pss[:], [(ones[:], sq[:, kc, :]) for kc in range(4)]), [ones, sq], [pss])
                rms_rstd(B, pss, rs, rstd, 512.0)
                for oc in range(4):
                    B.op("dve", lambda: V.scalar_tensor_tensor(out=ob[:, oc, :], in0=o[:, oc, :], scalar=nws[:, oc:oc + 1], in1=rstd[:],
                                                               op0=ALU.mult, op1=ALU.mult), [o, nws, rstd], [ob])
                B.dma(mixT[512:1024, t0:t0 + 512].rearrange("(c p) t -> p c t", p=128), ob[:], reads=[ob], q="act")

        front(0)
        for i in range(len(items)):
            if i + 1 < len(items):
                front(i + 1)
            mid0(i)
            mid(i)
            back(i)
            yield


def st_outproj(B, l, x_src, x_dst):
    nc = B.nc
    mixT = B.d["mixT"]
    with ExitStack() as es:
        wv = B.d["w_out"][l].rearrange("(kc p) n -> p kc n", p=128)
        W = [B.sb(es, "wo", [128, 1024], BF16) for _ in range(12)]
        for kc in range(12):
            B.dma(W[kc][:], wv[:, kc, :], writes=[W[kc]], q="pool")
        M = [B.sb(es, "M", [128, 12, 512], BF16) for _ in range(2)]
        X = [B.sb(es, "X", [128, 8, 512]) for _ in range(2)]
        pm = [B.ps(es, "pm") for _ in range(4)]
        xv = x_src.rearrange("(kc p) t -> p kc t", p=128)
        xo = x_dst.rearrange("(kc p) t -> p kc t", p=128)
        mv = mixT.rearrange("(kc p) t -> p kc t", p=128)
        for tt in range(B.TOK // 512):
            ts = slice(tt * 512, (tt + 1) * 512)
            Mt, Xt = M[tt % 2], X[tt % 2]
            B.dma(Mt[:], mv[:, :, ts], writes=[Mt])
            B.dma(Xt[:], xv[:, :, ts], writes=[Xt])
            for oc in range(8):
                P = pm[oc % 4]
                B.op("pe", lambda: mm_acc(nc, P[:], [(W[kc][:, oc * 128:(oc + 1) * 128], Mt[:, kc, :]) for kc in range(12)]), [Mt] + W, [P])
                B.op("dve", lambda: nc.vector.tensor_tensor(Xt[:, oc, :], Xt[:, oc, :], P[:], ALU.add), [Xt, P], [Xt])
            B.dma(xo[:, :, ts], Xt[:], reads=[Xt])
    B.S.barrier()


def st_ffn_up(B, l, x_src):
    nc = B.nc
    L = B.L
    gT = B.d["gT"]
    with ExitStack() as es:
        wv = B.d["ffn_w_up"][l].rearrange("(kc p) n -> p kc n", p=128)
        W = [B.sb(es, "wu", [128, 5632], BF16) for _ in range(8)]
        for kc in range(8):
            B.dma(W[kc][:], wv[:, kc, :], writes=[W[kc]], q="pool")
        nw = B.sb(es, "nw", [128, 8]); B.dma(nw[:], B.d["ffn_norm_w"][l], writes=[nw])
        cw = B.sb(es, "cw", [128, 44, 3]); B.dma(cw[:], B.d["ffn_conv_w"][l], writes=[cw])
        cb = B.sb(es, "cb", [128, 44]); B.dma(cb[:], B.d["ffn_conv_b"][l], writes=[cb])
        ones = B.sb(es, "ones", [128, 128], BF16)
        B.op("pool", lambda: nc.gpsimd.memset(ones[:], 1.0), [], [ones])
        xt = [B.sb(es, "xt", [128, 8, 512]) for _ in range(2)]
        hT = [B.sb(es, "hT", [128, 8, 512], BF16) for _ in range(2)]
        sq = B.sb(es, "sq", [128, 8, 512], BF16)
        rs = B.sb(es, "rs", [128, 512]); rstd = B.sb(es, "rstd", [128, 512])
        pss = B.ps(es, "pss")
        pm = [B.ps(es, "pm") for _ in range(4)]
        U = [B.sb(es, "U", [128, 514]) for _ in range(4)]
        tails = [B.sb(es, "tail", [128, 44, 2]) for _ in range(2)]
        cv = [B.sb(es, "cv", [128, 512]) for _ in range(4)]
        sg = [B.sb(es, "sg", [128, 512]) for _ in range(2)]
        G = [B.sb(es, "G", [128, 512], BF16) for _ in range(2)]
        k = 0
        for tt in range(B.TOK // 512):
            X, H = xt[tt % 2], hT[tt % 2]
            ts = slice(tt * 512, (tt + 1) * 512)
            first = (tt * 512) % L == 0
            told, tnew = tails[tt % 2], tails[(tt + 1) % 2]
            load_norm_tile(B, es, x_src, nw, ones, tt, X, H, sq, pss, rs, rstd)
            for fc in range(22):
                res = []
                for half in range(2):
                    ch = fc + 22 * half
                    P, Ut, C = pm[k % 4], U[k % 4], cv[k % 4]
                    k += 1
                    B.op("pe", lambda: mm_acc(nc, P[:], [(W[kc][:, ch * 128:(ch + 1) * 128], H[:, kc, :]) for kc in range(8)]), [H] + W, [P])
                    B.op("act", lambda: nc.scalar.copy(Ut[:, 2:514], P[:]), [P], [Ut])
                    B.op("act", lambda: nc.scalar.copy(tnew[:, ch, :], P[:, 510:512]), [P], [tnew])
                    B.op("act", lambda: nc.scalar.activation(C[:], P[:], AF.Identity, bias=cb[:, ch:ch + 1], scale=cw[:, ch, 2:3]), [P, cb, cw], [C])
                    if first:
                        B.op("dve", lambda: nc.vector.memset(Ut[:, 0:2], 0.0), [], [Ut])
                    else:
                        B.op("dve", lambda: nc.vector.tensor_copy(Ut[:, 0:2], told[:, ch, :]), [told], [Ut])
                    B.op("dve", lambda: nc.vector.scalar_tensor_tensor(out=C[:], in0=Ut[:, 0:512], scalar=cw[:, ch, 0:1], in1=C[:], op0=ALU.mult, op1=ALU.add), [Ut, cw, C], [C])
                    B.op("dve", lambda: nc.vector.scalar_tensor_tensor(out=C[:], in0=Ut[:, 1:513], scalar=cw[:, ch, 1:2], in1=C[:], op0=ALU.mult, op1=ALU.add), [Ut, cw, C], [C])
                    res.append(C)
                Sg, Gt = sg[fc % 2], G[fc % 2]
                B.op("act", lambda: nc.scalar.activation(Sg[:], res[0][:], AF.Silu), [res[0]], [Sg])
                B.op("dve", lambda: nc.vector.tensor_tensor(Gt[:], Sg[:], res[1][:], ALU.mult), [Sg, res[1]], [Gt])
                B.dma(gT[fc * 128:(fc + 1) * 128, ts], Gt[:], reads=[Gt])
    B.S.barrier()


def st_ffn_down(B, l, x_src, x_dst, final):
    nc = B.nc
    gT = B.d["gT"]
    with ExitStack() as es:
        wv = B.d["ffn_w_down"][l].rearrange("(kc p) n -> p kc n", p=128)
        W = [B.sb(es, "wd", [128, 1024], BF16) for _ in range(22)]
        for kc in range(22):
            B.dma(W[kc][:], wv[:, kc, :], writes=[W[kc]], q="pool")
        Gt = [B.sb(es, "G", [128, 22, 512], BF16) for _ in range(2)]
        X = [B.sb(es, "X", [128, 8, 512]) for _ in range(2)]
        pm = [B.ps(es, "pm") for _ in range(4)]
        xv = x_src.rearrange("(kc p) t -> p kc t", p=128)
        xo = x_dst.rearrange("(kc p) t -> p kc t", p=128)
        gv = gT.rearrange("(kc p) t -> p kc t", p=128)
        if final:
            nw = B.sb(es, "nw", [128, 8]); B.dma(nw[:], B.d["final_norm_w"], writes=[nw])
            ones = B.sb(es, "ones", [128, 128], BF16)
            B.op("pool", lambda: nc.gpsimd.memset(ones[:], 1.0), [], [ones])
            sq = B.sb(es, "sq", [128, 8, 512], BF16)
            rs = B.sb(es, "rs", [128, 512]); rstd = B.sb(es, "rstd", [128, 512])
            pss = B.ps(es, "pss")
            Y = [B.sb(es, "Y", [128, 8, 512]) for _ in range(2)]
        for tt in range(B.TOK // 512):
            ts = slice(tt * 512, (tt + 1) * 512)
            Gc, Xt = Gt[tt % 2], X[tt % 2]
            B.dma(Gc[:], gv[:, :, ts], writes=[Gc])
            B.dma(Xt[:], xv[:, :, ts], writes=[Xt])
            for oc in range(8):
                P = pm[oc % 4]
                B.op("pe", lambda: mm_acc(nc, P[:], [(W[kc][:, oc * 128:(oc + 1) * 128], Gc[:, kc, :]) for kc in range(22)]), [Gc] + W, [P])
                B.op("dve", lambda: nc.vector.tensor_tensor(Xt[:, oc, :], Xt[:, oc, :], P[:], ALU.add), [Xt, P], [Xt])
            if not final:
                B.dma(xo[:, :, ts], Xt[:], reads=[Xt])
            else:
                Yt = Y[tt % 2]
                B.op("act", lambda: nc.scalar.activation(sq[:], Xt[:], AF.Square), [Xt], [sq])
                B.op("pe", lambda: mm_acc(nc, pss[:], [(ones[:], sq[:, kc, :]) for kc in range(8)]), [ones, sq], [pss])
                rms_rstd(B, pss, rs, rstd, 1024.0)
                for kc in range(8):
                    B.op("dve", lambda: nc.vector.scalar_tensor_tensor(out=Yt[:, kc, :], in0=Xt[:, kc, :], scalar=nw[:, kc:kc + 1],
                                                                       in1=rstd[:], op0=ALU.mult, op1=ALU.mult), [Xt, nw, rstd], [Yt])
                B.dma(xo[:, :, ts], Yt[:], reads=[Yt])
    B.S.barrier()


def run_parallel(gens):
    gens = list(gens)
    while gens:
        for item in list(gens):
            g, w = item
            try:
                for _ in range(w):
                    next(g)
            except StopIteration:
                gens.remove(item)


def build(NS, L, dbg=False, depth=DEPTH, stages=None):
    B = Bld(NS, L, dbg, depth)
    TOK = B.TOK
    B.inp("x_in", [1024, TOK])
    B.inp("attn_norm_w", [DEPTH, 128, 8]); B.inp("w_in", [DEPTH, 1024, 4104])
    B.inp("diff_lambda_q1", [DEPTH, 64]); B.inp("diff_lambda_k1", [DEPTH, 64])
    B.inp("diff_lambda_q2", [DEPTH, 64]); B.inp("diff_lambda_k2", [DEPTH, 64])
    B.inp("diff_norm_w", [DEPTH, 128])
    B.inp("gdn_conv_w", [DEPTH, 128, 12, 4]); B.inp("gdn_a_log", [DEPTH, 4]); B.inp("gdn_dt_bias", [DEPTH, 4]); B.inp("gdn_norm_w", [DEPTH, 128, 1])
    B.inp("s5_lre", [DEPTH, 128, 32]); B.inp("s5_lim", [DEPTH, 128, 32]); B.inp("s5_log_dt", [DEPTH, 32])
    B.inp("s5_XA", [DEPTH, 128, 32, 16]); B.inp("s5_XB", [DEPTH, 128, 32, 16])
    B.inp("s5_CA", [DEPTH, 128, 32, 16]); B.inp("s5_CB", [DEPTH, 128, 32, 16])
    B.inp("s5_d", [DEPTH, 128, 4]); B.inp("s5_b_glu", [DEPTH, 128, 4]); B.inp("s5_norm_w", [DEPTH, 128, 4])
    B.inp("s5_w_glu", [DEPTH, 512, 512])
    B.scratch("s5_ctab", [32, 128, L]); B.scratch("s5_stab", [32, 128, L])
    B.inp("w_out", [DEPTH, 1536, 1024]); B.inp("ffn_norm_w", [DEPTH, 128, 8])
    B.inp("ffn_w_up", [DEPTH, 1024, 5632]); B.inp("ffn_conv_w", [DEPTH, 128, 44, 3]); B.inp("ffn_conv_b", [DEPTH, 128, 44])
    B.inp("ffn_w_down", [DEPTH, 2816, 1024]); B.inp("final_norm_w", [128, 8])
    B.scratch("proj", [PROJ_ROWS, TOK]); B.scratch("dv_tm", [TOK, 512], BF16)
    B.scratch("mixT", [1536, TOK], BF16); B.scratch("gT", [2816, TOK], BF16)
    B.scratch("xa", [1024, TOK]); B.scratch("xb", [1024, TOK])
    B.scratch("y", [1024, TOK], out=True)
    x_src = B.d["x_in"]
    for l in range(depth):
        with ExitStack() as esP:
            want = lambda nm: stages is None or nm in stages
            ctx = s5_persist(B, esP) if want("s5") else None
            with ExitStack() as es0:
                gens = []
                if want("inproj"):
                    gens.append((st_inproj(B, l, x_src, es0), 1))
                if want("s5"):
                    gens.append((st_s5_setup(B, l, es0, ctx), 2))
                run_parallel(gens)
                B.S.barrier()
            with ExitStack() as es1:
                gens = []
                if want("gdn"):
                    gens.append((st_gdn(B, l, es1, 8), 1))
                run_parallel(gens)
                B.S.barrier()
            if want("s5"):
                with ExitStack() as es2:
                    run_parallel([(st_s5_main(B, l, es2, ctx), 1)])
                    B.S.barrier()
            if want("attn"):
                with ExitStack() as es3:
                    run_parallel([(st_attn(B, l, es3), 1)])
                    B.S.barrier()
        if stages is None or "outproj" in stages:
            st_outproj(B, l, x_src, B.d["xa"])
        if stages is None or "ffn" in stages:
            st_ffn_up(B, l, B.d["xa"])
            last = l == depth - 1
            st_ffn_down(B, l, B.d["xa"], B.d["y"] if last else B.d["xb"], last)
        x_src = B.d["xb"]
    B.S.finish()
    return B


def prep_weights(inp):
    f = lambda a: np.ascontiguousarray(np.asarray(a, dtype=np.float32))
    w = {}
    perm = np.concatenate([np.arange(0, 2048), np.arange(2056, 2056 + 512 * 4), np.arange(2048, 2056)])
    w["w_in"] = f(inp["w_in"][:, :, perm])
    w["attn_norm_w"] = f(inp["attn_norm_w"].reshape(DEPTH, 8, 128).transpose(0, 2, 1))
    w["ffn_norm_w"] = f(inp["ffn_norm_w"].reshape(DEPTH, 8, 128).transpose(0, 2, 1))
    w["final_norm_w"] = f(inp["final_norm_w"].reshape(8, 128).transpose(1, 0))
    for k in ["diff_lambda_q1", "diff_lambda_k1", "diff_lambda_q2", "diff_lambda_k2", "diff_norm_w", "w_out", "ffn_w_up", "ffn_w_down"]:
        w[k] = f(inp[k])
    dup = lambda a: np.concatenate([a, a], axis=1)
    w["s5_lre"] = f(dup(inp["s5_lambda_re"].transpose(0, 2, 1)))
    w["s5_lim"] = f(dup(inp["s5_lambda_im"].transpose(0, 2, 1)))
    w["s5_log_dt"] = f(inp["s5_log_dt"])
    bre = inp["s5_b_re"].transpose(0, 2, 1, 3); bim = inp["s5_b_im"].transpose(0, 2, 1, 3)
    w["s5_XA"] = f(np.concatenate([bre, bim], axis=1)); w["s5_XB"] = f(np.concatenate([bim, bre], axis=1))
    cre = inp["s5_c_re"].transpose(0, 3, 1, 2); cim = inp["s5_c_im"].transpose(0, 3, 1, 2)
    w["s5_CA"] = f(np.concatenate([cre, cim], axis=1)); w["s5_CB"] = f(np.concatenate([cim, cre], axis=1))
    for k in ["s5_d", "s5_b_glu", "s5_norm_w"]:
        w[k] = f(inp[k].reshape(DEPTH, 4, 128).transpose(0, 2, 1))
    w["s5_w_glu"] = f(inp["s5_w_glu"])
    w["gdn_conv_w"] = f(inp["gdn_conv_w"].reshape(DEPTH, 4, 12, 128).transpose(0, 3, 2, 1))
    w["gdn_a_log"] = f(inp["gdn_a_log"]); w["gdn_dt_bias"] = f(inp["gdn_dt_bias"])
    w["gdn_norm_w"] = f(inp["gdn_norm_w"].reshape(DEPTH, 128, 1))
    w["ffn_conv_w"] = f(inp["ffn_conv_w"].reshape(DEPTH, 3, 44, 128).transpose(0, 3, 2, 1))
    w["ffn_conv_b"] = f(inp["ffn_conv_b"].reshape(DEPTH, 44, 128).transpose(0, 2, 1))
    return w


_CACHE = {}


def kernel(**inputs):
    x = np.asarray(inputs["x"], dtype=np.float32)
    Bsz, L, D = x.shape
    NS = Bsz // N_CORES
    key = (NS, L)
    if key not in _CACHE:
        _CACHE[key] = build(NS, L)
    B = _CACHE[key]
    w = prep_weights(inputs)
    in_maps = []
    for c in range(N_CORES):
        m = dict(w)
        xs = x[c * NS:(c + 1) * NS].reshape(NS * L, D)
        m["x_in"] = np.ascontiguousarray(xs.T)
        in_maps.append(m)
    res = run_bass_kernel_spmd(B.nc, in_maps, core_ids=list(range(N_CORES)))
    out = np.empty((Bsz, L, D), dtype=np.float32)
    for c in range(N_CORES):
        out[c * NS:(c + 1) * NS] = np.asarray(res.results[c]["y"]).T.reshape(NS, L, D)
    return out
```

```python
import math
from contextlib import ExitStack
import numpy as np
import concourse.bass as bass
import concourse.mybir as mybir
from concourse.bass_utils import run_bass_kernel_spmd

F32 = mybir.dt.float32
BF16 = mybir.dt.bfloat16
AF = mybir.ActivationFunctionType
ALU = mybir.AluOpType
AX = mybir.AxisListType
EPS = 1e-6
DEPTH = 2
N_CORES = 8
R_GQ, R_GK, R_GV, R_GZ, R_SU, R_DQ, R_DK, R_BA = 0, 512, 1024, 1536, 2048, 2560, 3072, 3584
C_DV, C_BA = 3584, 4096
PROJ_ROWS = 3592


class Res:
    __slots__ = ("w", "r")

    def __init__(self):
        self.w = None
        self.r = {}


class Tl:
    def __init__(self, h):
        self.h = h
        self.r = Res()

    def __getitem__(self, k):
        return self.h[k]


class Sched:
    NDS = 40

    def __init__(self, nc):
        self.nc = nc
        self.E = dict(pe=nc.tensor, act=nc.scalar, dve=nc.vector, pool=nc.gpsimd, sp=nc.sync)
        self.sem = {k: nc.alloc_semaphore("s_" + k) for k in self.E}
        self.cnt = {k: 0 for k in self.E}
        self.seen = {k: {} for k in self.E}
        self.dsem = [nc.alloc_semaphore("d%d" % i) for i in range(self.NDS)]
        self.dcnt = [0] * self.NDS
        self.dnext = 0

    def _semh(self, key):
        return self.dsem[key[1]] if isinstance(key, tuple) else self.sem[key]

    def wait(self, e, tok):
        key, val = tok
        if val <= 0 or self.seen[e].get(key, 0) >= val:
            return
        self.E[e].wait_ge(self._semh(key), val)
        self.seen[e][key] = val

    def _deps(self, reads, writes):
        deps = {}
        for r in reads:
            if r.w is not None:
                k, v = r.w
                deps[k] = max(deps.get(k, 0), v)
        for w in writes:
            if w.w is not None:
                k, v = w.w
                deps[k] = max(deps.get(k, 0), v)
            for k, v in w.r.items():
                deps[k] = max(deps.get(k, 0), v)
        return deps

    def _commit(self, tok, reads, writes):
        k, v = tok
        for r in reads:
            r.r[k] = v
        for w in writes:
            w.w = tok
            w.r = {}

    def op(self, e, fn, reads=(), writes=()):
        reads = [t.r for t in reads]
        writes = [t.r for t in writes]
        for k, v in self._deps(reads, writes).items():
            self.wait(e, (k, v))
        ins = fn()
        if isinstance(ins, (list, tuple)):
            ins = ins[-1]
        self.cnt[e] += 1
        ins.then_inc(self.sem[e], 1)
        self._commit((e, self.cnt[e]), reads, writes)

    def dma(self, out, in_, reads=(), writes=(), q="sp", **kw):
        reads = [t.r for t in reads]
        writes = [t.r for t in writes]
        i = self.dnext
        self.dnext = (i + 1) % self.NDS
        key = ("d", i)
        self.wait(q, (key, self.dcnt[i]))
        for k, v in self._deps(reads, writes).items():
            self.wait(q, (k, v))
        self.dcnt[i] += 16
        self.E[q].dma_start(out=out, in_=in_, **kw).then_inc(self.dsem[i], 16)
        self._commit((key, self.dcnt[i]), reads, writes)

    def barrier(self):
        for e in self.E:
            for f in self.E:
                self.wait(e, (f, self.cnt[f]))
            for i in range(self.NDS):
                self.wait(e, (("d", i), self.dcnt[i]))

    def finish(self):
        for i in range(self.NDS):
            self.wait("sp", (("d", i), self.dcnt[i]))


class Bld:
    def __init__(self, NS, L, dbg=False, depth=DEPTH):
        self.NS, self.L, self.TOK, self.dbg, self.depth = NS, L, NS * L, dbg, depth
        self.nc = bass.Bass("TRN2", target_bir_lowering=False)
        self.S = Sched(self.nc)
        self.uid = 0
        self.d = {}

    def inp(self, name, shape, dt=F32):
        self.d[name] = self.nc.dram_tensor(name, list(shape), dt, kind="ExternalInput").ap()

    def scratch(self, name, shape, dt=F32, out=False):
        kind = "ExternalOutput" if (out or self.dbg) else "Internal"
        self.d[name] = self.nc.dram_tensor(name, list(shape), dt, kind=kind).ap()
        return self.d[name]

    def sb(self, es, name, shape, dt=F32):
        self.uid += 1
        return Tl(es.enter_context(self.nc.sbuf_tensor("%s_%d" % (name, self.uid), list(shape), dt)))

    def ps(self, es, name, shape=(128, 512), dt=F32):
        self.uid += 1
        return Tl(es.enter_context(self.nc.psum_tensor("%s_%d" % (name, self.uid), list(shape), dt)))

    def op(self, e, fn, reads=(), writes=()):
        self.S.op(e, fn, reads, writes)

    def dma(self, out, in_, reads=(), writes=(), q="sp", **kw):
        self.S.dma(out, in_, reads, writes, q, **kw)


def mm_acc(nc, out, pairs):
    n = len(pairs)
    ins = None
    for i, (a, b) in enumerate(pairs):
        ins = nc.tensor.matmul(out, a, b, start=(i == 0), stop=(i == n - 1))
    return ins


def rms_rstd(B, pss, rs, rstd, n, eps=EPS):
    nc = B.nc
    B.op("act", lambda: nc.scalar.activation(rs[:], pss[:], AF.Sqrt, bias=eps, scale=1.0 / n), [pss], [rs])
    B.op("dve", lambda: nc.vector.reciprocal(rstd[:], rs[:]), [rs], [rstd])


def load_norm_tile(B, es, x_src, nw, ones, tt, X, H, sq, pss, rs, rstd):
    nc = B.nc
    ts = slice(tt * 512, (tt + 1) * 512)
    xv = x_src.rearrange("(kc p) t -> p kc t", p=128)
    B.dma(X[:], xv[:, :, ts], writes=[X])
    B.op("act", lambda: nc.scalar.activation(sq[:], X[:], AF.Square), [X], [sq])
    B.op("pe", lambda: mm_acc(nc, pss[:], [(ones[:], sq[:, kc, :]) for kc in range(8)]), [ones, sq], [pss])
    rms_rstd(B, pss, rs, rstd, 1024.0)
    for kc in range(8):
        B.op("dve", lambda: nc.vector.scalar_tensor_tensor(out=H[:, kc, :], in0=X[:, kc, :], scalar=nw[:, kc:kc + 1],
                                                           in1=rstd[:], op0=ALU.mult, op1=ALU.mult), [X, nw, rstd], [H])


def st_inproj(B, l, x_src, es):
    nc = B.nc
    proj, dv_tm = B.d["proj"], B.d["dv_tm"]
    if True:
        wv = B.d["w_in"][l].rearrange("(kc p) n -> p kc n", p=128)
        W = [B.sb(es, "win", [128, 4104], BF16) for _ in range(8)]
        for kc in range(8):
            B.dma(W[kc][:], wv[:, kc, :], writes=[W[kc]], q="pool")
        nw = B.sb(es, "nw", [128, 8])
        B.dma(nw[:], B.d["attn_norm_w"][l], writes=[nw])
        ones = B.sb(es, "ones", [128, 128], BF16)
        B.op("pool", lambda: nc.gpsimd.memset(ones[:], 1.0), [], [ones])
        xt = [B.sb(es, "xt", [128, 8, 512]) for _ in range(2)]
        hT = [B.sb(es, "hT", [128, 8, 512], BF16) for _ in range(2)]
        sq = B.sb(es, "sq", [128, 8, 512], BF16)
        rs = B.sb(es, "rs", [128, 512]); rstd = B.sb(es, "rstd", [128, 512])
        pss = B.ps(es, "pss")
        pm = [B.ps(es, "pm") for _ in range(4)]
        ob = [B.sb(es, "ob", [128, 512]) for _ in range(3)] + [None]
        ob[3] = ob[0]
        obv = [B.sb(es, "obv", [128, 512], BF16) for _ in range(2)]
        k = 0
        NT = B.TOK // 512
        load_norm_tile(B, es, x_src, nw, ones, 0, xt[0], hT[0], sq, pss, rs, rstd)
        for tt in range(NT):
            X, H = xt[tt % 2], hT[tt % 2]
            ts = slice(tt * 512, (tt + 1) * 512)
            for oc in range(29):
                if oc == 6 and tt + 1 < NT:
                    load_norm_tile(B, es, x_src, nw, ones, tt + 1, xt[(tt + 1) % 2], hT[(tt + 1) % 2], sq, pss, rs, rstd)
                P, O = pm[k % 4], ob[k % 4]
                m = 128 if oc < 28 else 8
                c0 = oc * 128 if oc < 28 else C_BA
                B.op("pe", lambda: mm_acc(nc, P[0:m, :], [(W[kc][:, c0:c0 + m], H[:, kc, :]) for kc in range(8)]), [H] + W, [P])
                B.op("act", lambda: nc.scalar.copy(O[0:m, :], P[0:m, :]), [P], [O])
                B.dma(proj[oc * 128:oc * 128 + m, ts], O[0:m, :], reads=[O])
                k += 1
                if oc % 2 == 1:
                    yield
            for tb in range(4):
                P, O = pm[k % 4], obv[tb % 2]
                B.op("pe", lambda: mm_acc(nc, P[:], [(H[:, kc, tb * 128:(tb + 1) * 128], W[kc][:, C_DV:C_DV + 512]) for kc in range(8)]), [H] + W, [P])
                B.op("act", lambda: nc.scalar.copy(O[:], P[:]), [P], [O])
                B.dma(dv_tm[tt * 512 + tb * 128: tt * 512 + (tb + 1) * 128, :], O[:], reads=[O])
                k += 1


def st_attn(B, l, es):
    nc = B.nc
    L, NS = B.L, B.NS
    NKB = L // 128
    proj, dv_tm, mixT = B.d["proj"], B.d["dv_tm"], B.d["mixT"]
    lam_init = 0.8 - 0.6 * math.exp(-0.3 * l)
    if True:
        lv = [B.sb(es, "lv", [128, 64]) for _ in range(4)]
        for i, nm in enumerate(["diff_lambda_q1", "diff_lambda_k1", "diff_lambda_q2", "diff_lambda_k2"]):
            B.dma(lv[i][:], B.d[nm][l].partition_broadcast(128), writes=[lv[i]])
        pr = B.sb(es, "pr", [128, 64]); s1 = B.sb(es, "s1", [128, 1]); s2 = B.sb(es, "s2", [128, 1])
        nlam = B.sb(es, "nlam", [128, 1])
        B.op("dve", lambda: nc.vector.tensor_tensor(pr[:], lv[0][:], lv[1][:], ALU.mult), [lv[0], lv[1]], [pr])
        B.op("dve", lambda: nc.vector.reduce_sum(s1[:], pr[:], AX.X), [pr], [s1])
        B.op("dve", lambda: nc.vector.tensor_tensor(pr[:], lv[2][:], lv[3][:], ALU.mult), [lv[2], lv[3]], [pr])
        B.op("dve", lambda: nc.vector.reduce_sum(s2[:], pr[:], AX.X), [pr], [s2])
        B.op("act", lambda: nc.scalar.activation(s1[:], s1[:], AF.Exp), [s1], [s1])
        B.op("act", lambda: nc.scalar.activation(s2[:], s2[:], AF.Exp), [s2], [s2])
        B.op("dve", lambda: nc.vector.tensor_tensor(nlam[:], s2[:], s1[:], ALU.subtract), [s1, s2], [nlam])
        B.op("dve", lambda: nc.vector.tensor_scalar(nlam[:], nlam[:], -lam_init, None, ALU.add), [nlam], [nlam])
        wrow = B.sb(es, "wrow", [128, 128])
        B.dma(wrow[:], B.d["diff_norm_w"][l].partition_broadcast(128), writes=[wrow])
        B.op("dve", lambda: nc.vector.tensor_scalar(wrow[:], wrow[:], 1.0 - lam_init, None, ALU.mult), [wrow], [wrow])
        onesf = B.sb(es, "onesf", [128, 128]); ident = B.sb(es, "ident", [128, 128])
        tri = B.sb(es, "tri", [128, 128], BF16)
        B.op("pool", lambda: nc.gpsimd.memset(onesf[:], 1.0), [], [onesf])
        B.op("pool", lambda: nc.gpsimd.affine_select(out=ident[:], in_=onesf[:], pattern=[[-1, 128]], compare_op=ALU.is_equal,
                                                     fill=0.0, base=0, channel_multiplier=1), [onesf], [ident])
        B.op("pool", lambda: nc.gpsimd.affine_select(out=tri[:], in_=onesf[:], pattern=[[1, 128]], compare_op=ALU.is_ge,
                                                     fill=0.0, base=0, channel_multiplier=-1), [onesf], [tri])
        Vs = [B.sb(es, "V", [128, NKB, 4, 132], BF16) for _ in range(min(2, NS))]
        qTs = [B.sb(es, "qT", [128, L], BF16) for _ in range(2)]; kTs = [B.sb(es, "kT", [128, L], BF16) for _ in range(2)]
        qst = B.sb(es, "qst", [128, L]); kst = B.sb(es, "kst", [128, L])
        psc = [B.ps(es, "psc", [128, 1024]) for _ in range(2)]
        pacc = [B.ps(es, "pacc") for _ in range(2)]
        ptr = B.ps(es, "ptr")
        Pt = [B.sb(es, "Pt", [128, 2, 512], BF16) for _ in range(2)]
        rc = B.sb(es, "rc", [128, 2]); o0 = B.sb(es, "o0", [128, 128]); a = B.sb(es, "a", [128, 128])
        junk = B.sb(es, "junk", [128, 128]); ssq = B.sb(es, "ssq", [128, 1]); rs = B.sb(es, "rs", [128, 1])
        rstd = B.sb(es, "rstd", [128, 1]); an = B.sb(es, "an", [128, 128])
        ot = [B.sb(es, "ot", [128, 512], BF16) for _ in range(2)]
        for s in range(NS):
            V = Vs[s % len(Vs)]
            B.op("pool", lambda: nc.gpsimd.memset(V[:], 1.0), [], [V])
            for hh in range(4):
                B.dma(V[:, :, hh, 0:128], dv_tm[s * L:(s + 1) * L, hh * 128:(hh + 1) * 128].rearrange("(kb p) e -> p kb e", p=128), writes=[V])
            for h in range(4):
                qT, kT = qTs[h % 2], kTs[h % 2]
                B.dma(qst[:], proj[R_DQ + h * 128:R_DQ + (h + 1) * 128, s * L:(s + 1) * L], writes=[qst])
                B.dma(kst[:], proj[R_DK + h * 128:R_DK + (h + 1) * 128, s * L:(s + 1) * L], writes=[kst])
                B.op("act", lambda: nc.scalar.copy(qT[:], qst[:]), [qst], [qT])
                B.op("act", lambda: nc.scalar.copy(kT[:], kst[:]), [kst], [kT])
                items = [(qb, g) for qb in range(NKB) for g in range(qb // 4 + 1)]

                def score(i):
                    qb, g = items[i]
                    kb0 = g * 4; nk = min(4, qb + 1 - kb0); Sp = psc[i % 2]
                    B.op("pe", lambda: [nc.tensor.matmul(Sp[:, c * 512 + j * 128:c * 512 + (j + 1) * 128], kT[c * 64:(c + 1) * 64, (kb0 + j) * 128:(kb0 + j + 1) * 128],
                                                         qT[c * 64:(c + 1) * 64, qb * 128:(qb + 1) * 128], start=True, stop=True) for c in (0, 1) for j in range(nk)],
                         [kT, qT], [Sp])

                def expmask(i):
                    qb, g = items[i]
                    kb0 = g * 4; nk = min(4, qb + 1 - kb0); Sp = psc[i % 2]; P = Pt[i % 2]
                    Spv = Sp[:, :].rearrange("p (c k) -> p c k", c=2)
                    B.op("act", lambda: nc.scalar.activation(P[:, :, 0:nk * 128], Spv[:, :, 0:nk * 128], AF.Exp, scale=0.125), [Sp], [P])
                    if kb0 + nk - 1 == qb:
                        B.op("dve", lambda: nc.vector.tensor_tensor(P[:, :, (nk - 1) * 128:nk * 128], P[:, :, (nk - 1) * 128:nk * 128],
                                                                    tri[:].unsqueeze(1).to_broadcast([128, 2, 128]), ALU.mult), [P, tri], [P])

                def pv(i):
                    qb, g = items[i]
                    kb0 = g * 4; nk = min(4, qb + 1 - kb0); P = Pt[i % 2]
                    acc = pacc[qb % 2]
                    B.op("pe", lambda: [nc.tensor.matmul(acc[:, c * 256:c * 256 + 129], P[:, c, j * 128:(j + 1) * 128], V[:, kb0 + j, h, 0:129],
                                                         start=(c == 0 and kb0 + j == 0), stop=(kb0 + j == qb), skip_group_check=True)
                                        for j in range(nk) for c in (0, 1)], [P, V], [acc])
                    if pend[1] is not None:
                        fin_b(pend[1])
                        pend[1] = None
                    if pend[0] is not None:
                        fin(pend[0])
                        pend[1] = pend[0]
                        pend[0] = None
                    if kb0 + nk - 1 == qb:
                        pend[0] = qb

                def fin(qb):
                    a0 = a1 = pacc[qb % 2]
                    V_ = nc.vector
                    B.op("dve", lambda: V_.reciprocal(rc[:, 0:1], a0[:, 128:129]), [a0], [rc])
                    B.op("dve", lambda: V_.reciprocal(rc[:, 1:2], a1[:, 384:385]), [a1], [rc])
                    B.op("dve", lambda: V_.tensor_tensor(rc[:, 1:2], rc[:, 1:2], nlam[:], ALU.mult), [rc, nlam], [rc])
                    B.op("dve", lambda: V_.tensor_scalar(o0[:], a0[:, 0:128], rc[:, 0:1], None, ALU.mult), [a0, rc], [o0])
                    B.op("dve", lambda: V_.scalar_tensor_tensor(out=a[:], in0=a1[:, 256:384], scalar=rc[:, 1:2], in1=o0[:],
                                                                op0=ALU.mult, op1=ALU.add), [a1, rc, o0], [a])
                    B.op("dve", lambda: V_.scalar_tensor_tensor(out=junk[:], in0=a[:], scalar=1.0, in1=a[:], op0=ALU.mult, op1=ALU.mult,
                                                                accum_out=ssq[:]), [a], [junk, ssq])
                    B.op("act", lambda: nc.scalar.activation(rs[:], ssq[:], AF.Ln, bias=EPS, scale=1.0 / 128), [ssq], [rs])
                    B.op("act", lambda: nc.scalar.activation(rstd[:], rs[:], AF.Exp, scale=-0.5), [rs], [rstd])
                    B.op("dve", lambda: V_.scalar_tensor_tensor(out=an[:], in0=a[:], scalar=rstd[:], in1=wrow[:],
                                                                op0=ALU.mult, op1=ALU.mult), [a, rstd, wrow], [an])

                def fin_b(qb):
                    O = ot[(qb // 4) % 2]
                    B.op("pe", lambda: nc.tensor.transpose(ptr[:, 0:128], an[:], ident[:]), [an, ident], [ptr])
                    B.op("act", lambda: nc.scalar.copy(O[:, (qb % 4) * 128:(qb % 4 + 1) * 128], ptr[:, 0:128]), [ptr], [O])
                    if qb % 4 == 3:
                        t0 = s * L + (qb // 4) * 512
                        B.dma(mixT[1024 + h * 128:1024 + (h + 1) * 128, t0:t0 + 512], O[:], reads=[O], q="act")

                pend = [None, None]
                score(0)
                for i in range(len(items)):
                    expmask(i)
                    if i + 1 < len(items):
                        score(i + 1)
                    pv(i)
                    yield
                if pend[1] is not None:
                    fin_b(pend[1])
                if pend[0] is not None:
                    fin(pend[0])
                    fin_b(pend[0])


def bc3(ap2, n):
    return ap2.unsqueeze(2).to_broadcast([ap2.shape[0], ap2.shape[1], n])


def bcm(ap2, n):
    return ap2.unsqueeze(1).to_broadcast([ap2.shape[0], n, ap2.shape[1]])


def st_gdn(B, l, es, nbanks=8):
    nc = B.nc
    L, NS = B.L, B.NS
    NC = L // 64
    NB = L // 512
    proj, mixT = B.d["proj"], B.d["mixT"]
    V, A, G = nc.vector, nc.scalar, nc.gpsimd
    if True:
        pbank = [B.ps(es, "pb") for _ in range(nbanks)]
        pbi = [0]

        npre = nbanks - 3
        pri = [0]

        def nbp():
            pbi[0] += 1
            return pbank[pbi[0] % npre]

        def nbr():
            pri[0] += 1
            return pbank[npre + pri[0] % 3]
        nb = nbp
        onesf = B.sb(es, "onesf", [128, 128]); ident = B.sb(es, "ident", [128, 128])
        B.op("pool", lambda: G.memset(onesf[:], 1.0), [], [onesf])
        B.op("pool", lambda: G.affine_select(out=ident[:], in_=onesf[:], pattern=[[-1, 128]], compare_op=ALU.is_equal,
                                             fill=0.0, base=0, channel_multiplier=1), [onesf], [ident])

        def mask64(name, pattern, cm, base, scale=None):
            t = B.sb(es, name, [64, 64])
            B.op("pool", lambda: G.affine_select(out=t[:], in_=onesf[0:64, 0:64], pattern=pattern, compare_op=ALU.is_ge,
                                                 fill=0.0, base=base, channel_multiplier=cm), [onesf], [t])
            if scale is not None:
                B.op("dve", lambda: V.tensor_scalar(t[:], t[:], scale, None, ALU.mult), [t], [t])
            return t
        U64 = mask64("U64", [[1, 64]], -1, 0)
        nU64 = mask64("nU64", [[1, 64]], -1, 0, -1.0)
        trilI = mask64("trilI", [[-1, 64]], 1, 0)
        strictL = mask64("strictL", [[-1, 64]], 1, -1)
        nstrictU = mask64("nstrictU", [[1, 64]], -1, -1, -1.0)
        nones64 = B.sb(es, "nones64", [64, 64])
        B.op("pool", lambda: G.memset(nones64[:], -1.0), [], [nones64])
        cw = B.sb(es, "cw", [128, 12, 4]); B.dma(cw[:], B.d["gdn_conv_w"][l], writes=[cw])
        gnw = B.sb(es, "gnw", [128, 1]); B.dma(gnw[:], B.d["gdn_norm_w"][l], writes=[gnw])
        nA = B.sb(es, "nA", [64, 4]); dtb = B.sb(es, "dtb", [64, 4])
        B.dma(nA[:], B.d["gdn_a_log"][l].partition_broadcast(64), writes=[nA])
        B.dma(dtb[:], B.d["gdn_dt_bias"][l].partition_broadcast(64), writes=[dtb])
        B.op("act", lambda: A.activation(nA[:], nA[:], AF.Exp), [nA], [nA])
        B.op("dve", lambda: V.tensor_scalar(nA[:], nA[:], -1.0, None, ALU.mult), [nA], [nA])
        ba = B.sb(es, "ba", [8, L])
        beta = B.sb(es, "beta", [64, NC, 4]); nbeta = B.sb(es, "nbeta", [64, NC, 4]); g_tm = B.sb(es, "g_tm", [64, NC, 4])
        gc_tm = B.sb(es, "gc_tm", [64, NC, 4]); egc = B.sb(es, "egc", [64, NC, 4]); ekd = B.sb(es, "ekd", [64, NC, 4])
        bg = B.sb(es, "bg", [64, NC, 4]); egl = B.sb(es, "egl", [128, NC, 4]); gl = B.sb(es, "gl", [128, NC, 4])
        raw = B.sb(es, "raw", [128, L + 3]); acc = B.sb(es, "acc", [128, L]); sqt = acc
        qn = B.sb(es, "qn", [128, L]); kn = B.sb(es, "kn", [128, L]); vs = B.sb(es, "vs", [128, L])
        rs5 = B.sb(es, "rs5", [128, 512]); rstd5 = B.sb(es, "rstd5", [128, 512])

        def t3(name, n=64):
            return B.sb(es, name, [64, 8, n])
        def t3b(name, n=64):
            return B.sb(es, name, [64, 8, n], BF16)
        k_tm, v_tm, o, on_ = t3("k_tm", 128), t3("v_tm", 128), t3("o", 128), t3("on", 128)
        kbg, vb = t3b("kbg", 128), t3b("vb", 128)
        G1, G2, dg, MbL, X0f, Z0f = t3("G1"), t3("G2"), t3("dg"), t3("MbL"), t3("X0f"), t3("Z0f")
        dec, decT = G1, G2
        X0, Z0, Xb, Zb, Qa, Qb = t3b("X0"), t3b("Z0"), t3b("Xb"), t3b("Zb"), t3b("Qa"), t3b("Qb")
        u2 = [t3("u", 128) for _ in range(2)]; kd2 = [t3b("kd", 128) for _ in range(2)]; attnT2 = [t3b("attnT") for _ in range(2)]
        wT2 = [B.sb(es, "wT", [128, 512], BF16) for _ in range(2)]; qgT2 = [B.sb(es, "qgT", [128, 512], BF16) for _ in range(2)]
        zt = B.sb(es, "zt", [128, 512])
        fin = B.sb(es, "fin", [128, 512], BF16)
        knb = B.sb(es, "knb", [128, L], BF16); qnb = B.sb(es, "qnb", [128, L], BF16)
        Sst = [B.sb(es, "S", [128, 128]) for _ in range(2)]
        Sbf = [B.sb(es, "Sb", [128, 128], BF16) for _ in range(2)]
        vnew = B.sb(es, "vnew", [64, 128], BF16); ss = B.sb(es, "ss", [64, 8]); rso = B.sb(es, "rso", [64, 8])
        for s in range(NS):
            c0 = s * L
            B.dma(ba[:], proj[R_BA:R_BA + 8, c0:c0 + L], writes=[ba])
            pt = nb()
            B.op("pe", lambda: [nc.tensor.transpose(pt[0:64, c * 8:(c + 1) * 8], ba[0:8, c * 64:(c + 1) * 64], ident[0:8, 0:8]) for c in range(NC)], [ba, ident], [pt])
            ptv = pt[0:64, 0:NC * 8].rearrange("p (c e) -> p c e", e=8)
            B.op("act", lambda: A.activation(beta[:], ptv[:, :, 0:4], AF.Sigmoid), [pt], [beta])
            B.op("dve", lambda: V.tensor_scalar(nbeta[:], beta[:], -1.0, None, ALU.mult), [beta], [nbeta])
            B.op("dve", lambda: V.tensor_tensor(g_tm[:], ptv[:, :, 4:8], bcm(dtb[:], NC), ALU.add), [pt, dtb], [g_tm])
            B.op("act", lambda: A.activation(g_tm[:], g_tm[:], AF.Exp), [g_tm], [g_tm])
            B.op("act", lambda: A.activation(g_tm[:], g_tm[:], AF.Ln, bias=1.0), [g_tm], [g_tm])
            B.op("dve", lambda: V.tensor_tensor(g_tm[:], g_tm[:], bcm(nA[:], NC), ALU.mult), [g_tm, nA], [g_tm])
            gflat = g_tm[:].rearrange("p c h -> p (c h)")
            p1 = nb()
            B.op("pe", lambda: nc.tensor.matmul(p1[0:64, 0:NC * 4], U64[:], gflat, start=True, stop=True), [U64, g_tm], [p1])
            B.op("dve", lambda: V.tensor_copy(gc_tm[:].rearrange("p c h -> p (c h)"), p1[0:64, 0:NC * 4]), [p1], [gc_tm])
            p2 = nb()
            B.op("pe", lambda: nc.tensor.matmul(p2[:, 0:NC * 4], onesf[0:64, :], gflat, start=True, stop=True), [onesf, g_tm], [p2])
            B.op("dve", lambda: V.tensor_copy(gl[:].rearrange("p c h -> p (c h)"), p2[:, 0:NC * 4]), [p2], [gl])
            B.op("act", lambda: A.activation(egl[:], gl[:], AF.Exp), [gl], [egl])
            B.op("act", lambda: A.activation(egc[:], gc_tm[:], AF.Exp), [gc_tm], [egc])
            B.op("dve", lambda: V.tensor_tensor(ekd[:], gl[0:64], gc_tm[:], ALU.subtract), [gl, gc_tm], [ekd])
            B.op("act", lambda: A.activation(ekd[:], ekd[:], AF.Exp), [ekd], [ekd])
            B.op("dve", lambda: V.tensor_tensor(bg[:], beta[:], egc[:], ALU.mult), [beta, egc], [bg])
            for h in range(4):
                for (ci, dst, r0) in ((h, qn, R_GQ), (4 + h, kn, R_GK), (8 + h, vs, R_GV)):
                    B.op("pool", lambda: G.memset(raw[:, 0:3], 0.0), [], [raw])
                    B.dma(raw[:, 3:L + 3], proj[r0 + h * 128:r0 + (h + 1) * 128, c0:c0 + L], writes=[raw])
                    B.op("dve", lambda: V.tensor_scalar(acc[:], raw[:, 0:L], cw[:, ci, 0:1], None, ALU.mult), [raw, cw], [acc])
                    for j in range(1, 4):
                        B.op("dve", lambda: V.scalar_tensor_tensor(out=acc[:], in0=raw[:, j:j + L], scalar=cw[:, ci, j:j + 1], in1=acc[:],
                                                                   op0=ALU.mult, op1=ALU.add), [raw, cw, acc], [acc])
                    B.op("act", lambda: A.activation(dst[:], acc[:], AF.Silu), [acc], [dst])
                    yield
                for (dst, scl) in ((qn, 128.0 ** -0.5), (kn, 1.0)):
                    B.op("act", lambda: A.activation(sqt[:], dst[:], AF.Square), [dst], [sqt])
                    for tt in range(NB):
                        pn = nb()
                        B.op("pe", lambda: nc.tensor.matmul(pn[:], onesf[:], sqt[:, tt * 512:(tt + 1) * 512], start=True, stop=True), [onesf, sqt], [pn])
                        B.op("act", lambda: A.activation(rs5[:], pn[:], AF.Sqrt, bias=1e-6), [pn], [rs5])
                        B.op("dve", lambda: V.reciprocal(rstd5[:], rs5[:]), [rs5], [rstd5])
                        B.op("dve", lambda: V.scalar_tensor_tensor(out=dst[:, tt * 512:(tt + 1) * 512], in0=dst[:, tt * 512:(tt + 1) * 512], scalar=scl,
                                                                   in1=rstd5[:], op0=ALU.mult, op1=ALU.mult), [dst, rstd5], [dst])
                B.op("pool", lambda: G.tensor_copy(knb[:], kn[:]), [kn], [knb])
                B.op("pool", lambda: G.tensor_copy(qnb[:], qn[:]), [qn], [qnb])
                B.op("pool", lambda: G.memset(Sst[0][:], 0.0), [], [Sst[0]])
                B.op("pool", lambda: G.memset(Sbf[0][:], 0.0), [], [Sbf[0]])
                sist = [0]

                def pre(b):
                    bs = b * 512
                    cs = slice(b * 8, (b + 1) * 8)
                    u, wT, qgT, attnT, kd = u2[b % 2], wT2[b % 2], qgT2[b % 2], attnT2[b % 2], kd2[b % 2]

                    def hb(t):
                        return t[:, cs, h]
                    for (src, dst) in ((kn, k_tm), (vs, v_tm)):
                        for half in range(2):
                            pk = nbp()
                            B.op("pe", lambda: [nc.tensor.transpose(pk[0:64, q * 128:(q + 1) * 128], src[:, bs + (half * 4 + q) * 64: bs + (half * 4 + q + 1) * 64], ident[:])
                                                for q in range(4)], [src, ident], [pk])
                            B.op("act", lambda: A.copy(dst[:, half * 4:(half + 1) * 4, :].rearrange("p c e -> p (c e)"), pk[0:64, :]), [pk], [dst])
                    yield
                    B.op("dve", lambda: V.tensor_copy(G1[:], bc3(hb(g_tm), 64)), [g_tm], [G1])
                    B.op("dve", lambda: V.tensor_tensor(G2[:], bcm(U64[:], 8), bc3(hb(g_tm), 64), ALU.mult), [U64, g_tm], [G2])
                    G1f, G2f = G1[:].rearrange("p c e -> p (c e)"), G2[:].rearrange("p c e -> p (c e)")
                    pD, pDT = nbp(), nbp()
                    B.op("pe", lambda: [nc.tensor.matmul(pD[0:64, :], U64[:], G1f, start=True, stop=False),
                                        nc.tensor.matmul(pD[0:64, :], nones64[:], G2f, start=False, stop=True)], [U64, nones64, G1, G2], [pD])
                    B.op("pe", lambda: [nc.tensor.matmul(pDT[0:64, :], onesf[0:64, 0:64], G2f, start=True, stop=False),
                                        nc.tensor.matmul(pDT[0:64, :], nU64[:], G1f, start=False, stop=True)], [onesf, nU64, G1, G2], [pDT])
                    for (pp, dd, mk) in ((pD, dec, trilI), (pDT, decT, U64)):
                        ddf = dd[:].rearrange("p c e -> p (c e)")
                        B.op("dve", lambda: V.tensor_scalar(ddf, pp[0:64, :], 0.0, None, ALU.min), [pp], [dd])
                        B.op("act", lambda: A.activation(ddf, ddf, AF.Exp), [dd], [dd])
                        B.op("pool", lambda: G.tensor_tensor(dd[:], dd[:], bcm(mk[:], 8), ALU.mult), [dd, mk], [dd])
                    yield
                    pKK = nbp()
                    B.op("pe", lambda: [nc.tensor.matmul(pKK[0:64, c * 64:(c + 1) * 64], knb[:, bs + c * 64:bs + (c + 1) * 64], knb[:, bs + c * 64:bs + (c + 1) * 64], start=True, stop=True)
                                        for c in range(8)], [knb], [pKK])
                    B.op("pool", lambda: G.tensor_tensor(MbL[:], bcm(strictL[:], 8), bc3(hb(nbeta), 64), ALU.mult), [strictL, nbeta], [MbL])
                    B.op("dve", lambda: V.tensor_tensor(X0f[:].rearrange("p c e -> p (c e)"), pKK[0:64, :], dec[:].rearrange("p c e -> p (c e)"), ALU.mult), [pKK, dec], [X0f])
                    B.op("dve", lambda: V.tensor_tensor(X0[:], X0f[:], MbL[:], ALU.mult), [X0f, MbL], [X0])
                    B.op("dve", lambda: V.tensor_tensor(Z0f[:].rearrange("p c e -> p (c e)"), pKK[0:64, :], decT[:].rearrange("p c e -> p (c e)"), ALU.mult), [pKK, decT], [Z0f])
                    B.op("dve", lambda: V.tensor_tensor(dg[:], bcm(ident[0:64, 0:64], 8), bc3(hb(beta), 64), ALU.mult), [ident, beta], [dg])
                    pBR = nbp()
                    B.op("pe", lambda: nc.tensor.matmul(pBR[0:64, :], onesf[0:64, 0:64], dg[:].rearrange("p c e -> p (c e)"), start=True, stop=True), [onesf, dg], [pBR])
                    B.op("dve", lambda: V.tensor_tensor(Z0f[:].rearrange("p c e -> p (c e)"), Z0f[:].rearrange("p c e -> p (c e)"), pBR[0:64, :], ALU.mult), [Z0f, pBR], [Z0f])
                    B.op("pool", lambda: G.tensor_tensor(Z0f[:], Z0f[:], bcm(nstrictU[:], 8), ALU.mult), [Z0f, nstrictU], [Z0f])
                    B.op("pool", lambda: G.tensor_copy(Z0[:], Z0f[:]), [Z0f], [Z0])
                    pQK = nbp()
                    B.op("pe", lambda: [nc.tensor.matmul(pQK[0:64, c * 64:(c + 1) * 64], knb[:, bs + c * 64:bs + (c + 1) * 64], qnb[:, bs + c * 64:bs + (c + 1) * 64], start=True, stop=True)
                                        for c in range(8)], [knb, qnb], [pQK])
                    B.op("dve", lambda: V.tensor_tensor(attnT[:].rearrange("p c e -> p (c e)"), pQK[0:64, :], decT[:].rearrange("p c e -> p (c e)"), ALU.mult), [pQK, decT], [attnT])
                    B.op("pool", lambda: G.tensor_tensor(Qa[:], Z0f[:], bcm(ident[0:64, 0:64], 8), ALU.add), [Z0f, ident], [Qa])
                    Xc, Zc, Qc = X0, Z0, Qa
                    Xn, Zn, Qn = Xb, Zb, Qb
                    for lev in range(5):
                        pX, pZ = nbp(), nbp()
                        B.op("pe", lambda: [nc.tensor.matmul(pX[0:64, c * 64:(c + 1) * 64], Zc[:, c, :], Xc[:, c, :], start=True, stop=True) for c in range(8)], [Zc, Xc], [pX])
                        if lev < 4:
                            B.op("pe", lambda: [nc.tensor.matmul(pZ[0:64, c * 64:(c + 1) * 64], Xc[:, c, :], Zc[:, c, :], start=True, stop=True) for c in range(8)], [Zc, Xc], [pZ])
                        B.op("dve", lambda: V.tensor_copy(Xn[:].rearrange("p c e -> p (c e)"), pX[0:64, :]), [pX], [Xn])
                        if lev < 4:
                            B.op("act", lambda: A.copy(Zn[:].rearrange("p c e -> p (c e)"), pZ[0:64, :]), [pZ], [Zn])
                        pQ = nbp()
                        B.op("pe", lambda: [nc.tensor.matmul(pQ[0:64, c * 64:(c + 1) * 64], Xn[:, c, :], Qc[:, c, :], start=True, stop=True) for c in range(8)], [Xn, Qc], [pQ])
                        B.op("dve", lambda: V.tensor_tensor(Qn[:].rearrange("p c e -> p (c e)"), Qc[:].rearrange("p c e -> p (c e)"), pQ[0:64, :], ALU.add), [Qc, pQ], [Qn])
                        yield
                        Xc, Xn = Xn, Xc
                        Zc, Zn = Zn, Zc
                        Qc, Qn = Qn, Qc
                    TT = Qc
                    B.op("pool", lambda: G.tensor_tensor(kbg[:], k_tm[:], bc3(hb(bg), 128), ALU.mult), [k_tm, bg], [kbg])
                    B.op("pool", lambda: G.tensor_tensor(kd[:], k_tm[:], bc3(hb(ekd), 128), ALU.mult), [k_tm, ekd], [kd])
                    B.op("pool", lambda: G.tensor_tensor(vb[:], v_tm[:], bc3(hb(beta), 128), ALU.mult), [v_tm, beta], [vb])
                    for half in range(2):
                        pu = nbp()
                        B.op("pe", lambda: [nc.tensor.matmul(pu[0:64, q * 128:(q + 1) * 128], TT[:, half * 4 + q, :], vb[:, half * 4 + q, :], start=True, stop=True) for q in range(4)], [TT, vb], [pu])
                        B.op("act", lambda: A.copy(u[:, half * 4:(half + 1) * 4, :].rearrange("p c e -> p (c e)"), pu[0:64, :]), [pu], [u])
                    pw = nbp()
                    B.op("pe", lambda: [nc.tensor.matmul(pw[:, c * 64:(c + 1) * 64], kbg[:, c, :], TT[:, c, :], start=True, stop=True) for c in range(8)], [kbg, TT], [pw])
                    B.op("act", lambda: A.copy(wT[:], pw[:]), [pw], [wT])
                    B.op("dve", lambda: V.tensor_tensor(dg[:], bcm(ident[0:64, 0:64], 8), bc3(hb(egc), 64), ALU.mult), [ident, egc], [dg])
                    pe_ = nbp()
                    B.op("pe", lambda: nc.tensor.matmul(pe_[:], onesf[0:64, :], dg[:].rearrange("p c e -> p (c e)"), start=True, stop=True), [onesf, dg], [pe_])
                    B.op("dve", lambda: V.tensor_tensor(qgT[:], qn[:, bs:bs + 512], pe_[:], ALU.mult), [qn, pe_], [qgT])
                    yield

                def rec(b):
                    bs = b * 512
                    cs = slice(b * 8, (b + 1) * 8)
                    u, wT, qgT, attnT, kd = u2[b % 2], wT2[b % 2], qgT2[b % 2], attnT2[b % 2], kd2[b % 2]

                    def hb(t):
                        return t[:, cs, h]
                    for c in range(8):
                        Sc, Sn = Sst[sist[0] % 2], Sst[(sist[0] + 1) % 2]
                        Sbc, Sbn = Sbf[sist[0] % 2], Sbf[(sist[0] + 1) % 2]
                        sist[0] += 1
                        pW, pO, pS = nbr(), nbr(), nbr()
                        B.op("pe", lambda: nc.tensor.matmul(pW[0:64, 0:128], wT[:, c * 64:(c + 1) * 64], Sbc[:], start=True, stop=True), [wT, Sbc], [pW])
                        B.op("dve", lambda: V.tensor_tensor(vnew[:], u[:, c, :], pW[0:64, 0:128], ALU.subtract), [u, pW], [vnew])
                        B.op("pe", lambda: [nc.tensor.matmul(pO[0:64, 0:128], qgT[:, c * 64:(c + 1) * 64], Sbc[:], start=True, stop=False),
                                            nc.tensor.matmul(pO[0:64, 0:128], attnT[:, c, :], vnew[:], start=False, stop=True)], [qgT, Sbc, attnT, vnew], [pO])
                        B.op("pe", lambda: nc.tensor.matmul(pS[:, 0:128], kd[:, c, :], vnew[:], start=True, stop=True), [kd, vnew], [pS])
                        B.op("act", lambda: A.copy(o[:, c, :], pO[0:64, 0:128]), [pO], [o])
                        B.op("dve", lambda: V.scalar_tensor_tensor(out=Sbn[:], in0=Sc[:], scalar=egl[:, b * 8 + c, h:h + 1], in1=pS[:, 0:128],
                                                                   op0=ALU.mult, op1=ALU.add), [Sc, egl, pS], [Sbn])
                        B.op("dve", lambda: V.scalar_tensor_tensor(out=Sn[:], in0=Sc[:], scalar=egl[:, b * 8 + c, h:h + 1], in1=pS[:, 0:128],
                                                                   op0=ALU.mult, op1=ALU.add), [Sc, egl, pS], [Sn])
                        yield

                def epi(b):
                    bs = b * 512
                    cs = slice(b * 8, (b + 1) * 8)
                    u, wT, qgT, attnT, kd = u2[b % 2], wT2[b % 2], qgT2[b % 2], attnT2[b % 2], kd2[b % 2]

                    def hb(t):
                        return t[:, cs, h]
                    B.op("pool", lambda: G.tensor_tensor(on_[:], o[:], o[:], ALU.mult), [o], [on_])
                    B.op("dve", lambda: V.tensor_reduce(ss[:], on_[:], AX.X, ALU.add), [on_], [ss])
                    B.op("act", lambda: A.activation(rso[:], ss[:], AF.Sqrt, bias=EPS, scale=1.0 / 128), [ss], [rso])
                    B.op("dve", lambda: V.reciprocal(rso[:], rso[:]), [rso], [rso])
                    B.op("dve", lambda: V.tensor_tensor(on_[:], o[:], bc3(rso[:], 128), ALU.mult), [o, rso], [on_])
                    pT = nbp()
                    B.op("pe", lambda: [nc.tensor.transpose(pT[:, c * 64:(c + 1) * 64], on_[:, c, :], ident[0:64, 0:64]) for c in range(8)], [on_, ident], [pT])
                    B.dma(zt[:], proj[R_GZ + h * 128:R_GZ + (h + 1) * 128, c0 + bs:c0 + bs + 512], writes=[zt])
                    B.op("act", lambda: A.activation(zt[:], zt[:], AF.Silu), [zt], [zt])
                    B.op("dve", lambda: V.scalar_tensor_tensor(out=fin[:], in0=pT[:], scalar=gnw[:, 0:1], in1=zt[:], op0=ALU.mult, op1=ALU.mult), [pT, gnw, zt], [fin])
                    B.dma(mixT[h * 128:(h + 1) * 128, c0 + bs:c0 + bs + 512], fin[:], reads=[fin])

                def drive(*gs):
                    gs = [g for g in gs if g is not None]
                    while gs:
                        for g in list(gs):
                            try:
                                next(g)
                            except StopIteration:
                                gs.remove(g)
                        yield
                yield from drive(pre(0))
                for b in range(NB):
                    yield from drive(rec(b), pre(b + 1) if b + 1 < NB else None)
                    epi(b)
                    yield


def bc3(ap2, n):
    return ap2.unsqueeze(2).to_broadcast([ap2.shape[0], ap2.shape[1], n])


def s5_persist(B, es):
    ctx = {}
    ctx["r"] = B.sb(es, "r", [128, 32])
    for nm in ("L1", "L2", "CW1", "CW2"):
        ctx[nm] = B.sb(es, nm, [128, 32, 128], BF16)
    return ctx


def st_s5_setup(B, l, es, ctx):
    nc = B.nc
    L, NS, TOK = B.L, B.NS, B.TOK
    NLV = int(math.log2(L))
    proj, mixT, Ctab, Stab = B.d["proj"], B.d["mixT"], B.d["s5_ctab"], B.d["s5_stab"]
    V = nc.vector
    r, L1, L2, CW1, CW2 = ctx["r"], ctx["L1"], ctx["L2"], ctx["CW1"], ctx["CW2"]
    if True:
        def t32(name):
            return B.sb(es, name, [128, 32])
        lre, lim, dt, xr, th = t32("lre"), t32("lim"), t32("dt"), t32("xr"), t32("th")
        c, s_, c2, s2, tmp, tmp2 = t32("c"), t32("s"), t32("c2"), t32("s2"), t32("tmp"), t32("tmp2")
        nr, ni, den, cre, cim, cimA, creB = t32("nr"), t32("ni"), t32("den"), t32("cre"), t32("cim"), t32("cimA"), t32("creB")
        B.dma(lre[:], B.d["s5_lre"][l], writes=[lre]); B.dma(lim[:], B.d["s5_lim"][l], writes=[lim])
        B.dma(dt[:], B.d["s5_log_dt"][l].partition_broadcast(128), writes=[dt])
        sgA = B.sb(es, "sgA", [128, 1]); sgB = B.sb(es, "sgB", [128, 1])
        B.op("pool", lambda: [nc.gpsimd.memset(sgA[0:64, :], -1.0), nc.gpsimd.memset(sgA[64:128, :], 1.0)], [], [sgA])
        B.op("pool", lambda: [nc.gpsimd.memset(sgB[0:64, :], 1.0), nc.gpsimd.memset(sgB[64:128, :], -1.0)], [], [sgB])
        B.op("dve", lambda: V.tensor_scalar(lre[:], lre[:], -1e-4, None, ALU.min), [lre], [lre])
        B.op("act", lambda: nc.scalar.activation(dt[:], dt[:], AF.Exp), [dt], [dt])
        B.op("dve", lambda: V.tensor_tensor(xr[:], lre[:], dt[:], ALU.mult), [lre, dt], [xr])
        B.op("dve", lambda: V.tensor_tensor(th[:], lim[:], dt[:], ALU.mult), [lim, dt], [th])
        B.op("act", lambda: nc.scalar.activation(r[:], xr[:], AF.Exp), [xr], [r])
        B.op("act", lambda: nc.scalar.activation(s_[:], th[:], AF.Sin, scale=0.125), [th], [s_])
        B.op("act", lambda: nc.scalar.activation(c[:], th[:], AF.Sin, bias=math.pi / 2, scale=-0.125), [th], [c])

        def dbl(ci, si, co, so):
            B.op("dve", lambda: V.tensor_tensor(tmp[:], ci[:], ci[:], ALU.mult), [ci], [tmp])
            B.op("dve", lambda: V.tensor_tensor(tmp2[:], si[:], si[:], ALU.mult), [si], [tmp2])
            B.op("dve", lambda: V.tensor_tensor(so[:], ci[:], si[:], ALU.mult), [ci, si], [so])
            B.op("dve", lambda: V.tensor_tensor(co[:], tmp[:], tmp2[:], ALU.subtract), [tmp, tmp2], [co])
            B.op("dve", lambda: V.tensor_scalar(so[:], so[:], 2.0, None, ALU.mult), [so], [so])
        dbl(c, s_, c2, s2); dbl(c2, s2, c, s_); dbl(c, s_, c2, s2)
        CK = B.sb(es, "CK", [128, NLV, 32]); SK = B.sb(es, "SK", [128, NLV, 32]); NSK = B.sb(es, "NSK", [128, NLV, 32])
        B.op("dve", lambda: V.tensor_copy(CK[:, 0, :], c2[:]), [c2], [CK])
        B.op("dve", lambda: V.tensor_copy(SK[:, 0, :], s2[:]), [s2], [SK])
        for k in range(1, NLV):
            B.op("dve", lambda: V.tensor_tensor(tmp[:], CK[:, k - 1, :], CK[:, k - 1, :], ALU.mult), [CK], [tmp])
            B.op("dve", lambda: V.tensor_tensor(tmp2[:], SK[:, k - 1, :], SK[:, k - 1, :], ALU.mult), [SK], [tmp2])
            B.op("dve", lambda: V.tensor_tensor(CK[:, k, :], tmp[:], tmp2[:], ALU.subtract), [tmp, tmp2], [CK])
            B.op("dve", lambda: V.tensor_tensor(tmp[:], CK[:, k - 1, :], SK[:, k - 1, :], ALU.mult), [CK, SK], [tmp])
            B.op("dve", lambda: V.tensor_scalar(SK[:, k, :], tmp[:], 2.0, None, ALU.mult), [tmp], [SK])
        B.op("dve", lambda: V.tensor_scalar(NSK[:], SK[:], -1.0, None, ALU.mult), [SK], [NSK])
        B.op("dve", lambda: V.tensor_tensor(nr[:], r[:], c2[:], ALU.mult), [r, c2], [nr])
        B.op("dve", lambda: V.tensor_scalar(nr[:], nr[:], -1.0, None, ALU.add), [nr], [nr])
        B.op("dve", lambda: V.tensor_tensor(ni[:], r[:], s2[:], ALU.mult), [r, s2], [ni])
        B.op("dve", lambda: V.tensor_tensor(den[:], lre[:], lre[:], ALU.mult), [lre], [den])
        B.op("dve", lambda: V.tensor_tensor(tmp[:], lim[:], lim[:], ALU.mult), [lim], [tmp])
        B.op("dve", lambda: V.tensor_tensor(den[:], den[:], tmp[:], ALU.add), [den, tmp], [den])
        B.op("dve", lambda: V.reciprocal(den[:], den[:]), [den], [den])
        B.op("dve", lambda: V.tensor_tensor(tmp[:], nr[:], lre[:], ALU.mult), [nr, lre], [tmp])
        B.op("dve", lambda: V.tensor_tensor(tmp2[:], ni[:], lim[:], ALU.mult), [ni, lim], [tmp2])
        B.op("dve", lambda: V.tensor_tensor(cre[:], tmp[:], tmp2[:], ALU.add), [tmp, tmp2], [cre])
        B.op("dve", lambda: V.tensor_tensor(cre[:], cre[:], den[:], ALU.mult), [cre, den], [cre])
        B.op("dve", lambda: V.tensor_tensor(tmp[:], ni[:], lre[:], ALU.mult), [ni, lre], [tmp])
        B.op("dve", lambda: V.tensor_tensor(tmp2[:], nr[:], lim[:], ALU.mult), [nr, lim], [tmp2])
        B.op("dve", lambda: V.tensor_tensor(cim[:], tmp[:], tmp2[:], ALU.subtract), [tmp, tmp2], [cim])
        B.op("dve", lambda: V.tensor_tensor(cim[:], cim[:], den[:], ALU.mult), [cim, den], [cim])
        B.op("dve", lambda: V.tensor_scalar(cimA[:], cim[:], sgA[:, 0:1], None, ALU.mult), [cim, sgA], [cimA])
        B.op("dve", lambda: V.tensor_scalar(creB[:], cre[:], sgB[:, 0:1], None, ALU.mult), [cre, sgB], [creB])
        XA = B.sb(es, "XA", [128, 32, 16]); XB = B.sb(es, "XB", [128, 32, 16])
        B.dma(XA[:], B.d["s5_XA"][l], writes=[XA]); B.dma(XB[:], B.d["s5_XB"][l], writes=[XB])
        B1T = B.sb(es, "B1T", [128, 32, 16]); B2T = B.sb(es, "B2T", [128, 32, 16]); bt3 = B.sb(es, "bt3", [128, 32, 16])
        B.op("dve", lambda: V.tensor_tensor(B1T[:], XA[:], bc3(cre[:], 16), ALU.mult), [XA, cre], [B1T])
        B.op("dve", lambda: V.tensor_tensor(bt3[:], XB[:], bc3(cimA[:], 16), ALU.mult), [XB, cimA], [bt3])
        B.op("dve", lambda: V.tensor_tensor(B1T[:], B1T[:], bt3[:], ALU.add), [B1T, bt3], [B1T])
        B.op("dve", lambda: V.tensor_tensor(B2T[:], XB[:], bc3(creB[:], 16), ALU.mult), [XB, creB], [B2T])
        B.op("dve", lambda: V.tensor_tensor(bt3[:], XA[:], bc3(cim[:], 16), ALU.mult), [XA, cim], [bt3])
        B.op("dve", lambda: V.tensor_tensor(B2T[:], B2T[:], bt3[:], ALU.add), [B2T, bt3], [B2T])
        onesf = B.sb(es, "onesf", [128, 128]); ident = B.sb(es, "ident", [128, 128])
        m1 = B.sb(es, "m1", [128, 8]); mask8 = B.sb(es, "mask8", [128, 8])
        B.op("pool", lambda: nc.gpsimd.memset(onesf[:], 1.0), [], [onesf])
        B.op("pool", lambda: nc.gpsimd.affine_select(out=ident[:], in_=onesf[:], pattern=[[-1, 128]], compare_op=ALU.is_equal,
                                                     fill=0.0, base=0, channel_multiplier=1), [onesf], [ident])
        B.op("pool", lambda: nc.gpsimd.affine_select(out=m1[:], in_=onesf[:, 0:8], pattern=[[-16, 8]], compare_op=ALU.is_ge,
                                                     fill=0.0, base=0, channel_multiplier=1), [onesf], [m1])
        B.op("pool", lambda: nc.gpsimd.affine_select(out=mask8[:], in_=m1[:], pattern=[[16, 8]], compare_op=ALU.is_ge,
                                                     fill=0.0, base=15, channel_multiplier=-1), [m1], [mask8])
        ptr = B.ps(es, "PAB")
        yield
        for (src, dst) in ((B1T, L1), (B2T, L2)):
            for gc in range(4):
                B.op("pe", lambda: nc.tensor.transpose(ptr[:, 0:128], src[:, gc * 8:(gc + 1) * 8, :], ident[:]), [src, ident], [ptr])
                for j in range(8):
                    B.op("dve", lambda: V.tensor_scalar(dst[:, gc * 8 + j, :], ptr[:, 0:128], mask8[:, j:j + 1], None, ALU.mult), [ptr, mask8], [dst])
        CA, CB = XA, XB
        B.dma(CA[:], B.d["s5_CA"][l], writes=[CA]); B.dma(CB[:], B.d["s5_CB"][l], writes=[CB])
        yield
        B.op("pool", lambda: nc.gpsimd.memset(CW1[:], 0.0), [], [CW1])
        B.op("pool", lambda: nc.gpsimd.memset(CW2[:], 0.0), [], [CW2])
        for gc in range(4):
            for j in range(8):
                g = gc * 8 + j
                B.op("dve", lambda: V.tensor_scalar(CW1[:, g, 16 * j:16 * j + 16], CA[:, g, :], sgB[:, 0:1], None, ALU.mult), [CA, sgB], [CW1])
                B.op("dve", lambda: V.tensor_scalar(CW2[:, g, 16 * j:16 * j + 16], CB[:, g, :], -1.0, None, ALU.mult), [CB], [CW2])
        if True:
            es2 = es
            GB = 1
            tc_ = B.sb(es2, "tc", [128, GB, L]); ts_ = B.sb(es2, "ts", [128, GB, L])
            tq = B.sb(es2, "tq", [128, GB, L // 2]); tq2 = B.sb(es2, "tq2", [128, GB, L // 2])
            for gb in range(32 // GB):
                gs = slice(gb * GB, (gb + 1) * GB)
                B.op("dve", lambda: V.memset(tc_[:, :, 0:1], 1.0), [], [tc_])
                B.op("dve", lambda: V.memset(ts_[:, :, 0:1], 0.0), [], [ts_])
                for k in range(NLV):
                    n = 1 << k
                    ckp, skp, nskp = CK[:, k, gb:gb + 1], SK[:, k, gb:gb + 1], NSK[:, k, gb:gb + 1]
                    B.op("dve", lambda: V.tensor_scalar(tq[:, 0, 0:n], ts_[:, 0, 0:n], nskp, None, ALU.mult), [ts_, NSK], [tq])
                    B.op("dve", lambda: V.tensor_scalar(tq2[:, 0, 0:n], tc_[:, 0, 0:n], skp, None, ALU.mult), [tc_, SK], [tq2])
                    B.op("dve", lambda: V.scalar_tensor_tensor(out=tc_[:, 0, n:2 * n], in0=tc_[:, 0, 0:n], scalar=ckp, in1=tq[:, 0, 0:n],
                                                               op0=ALU.mult, op1=ALU.add), [tc_, CK, tq], [tc_])
                    B.op("dve", lambda: V.scalar_tensor_tensor(out=ts_[:, 0, n:2 * n], in0=ts_[:, 0, 0:n], scalar=ckp, in1=tq2[:, 0, 0:n],
                                                               op0=ALU.mult, op1=ALU.add), [ts_, CK, tq2], [ts_])
                    if k >= 6:
                        yield
                B.dma(Ctab[gs].rearrange("g p t -> p g t"), tc_[:], reads=[tc_])
                B.dma(Stab[gs].rearrange("g p t -> p g t"), ts_[:], reads=[ts_])
                yield


def st_s5_main(B, l, es, ctx):
    nc = B.nc
    L, NS, TOK = B.L, B.NS, B.TOK
    proj, mixT, Ctab, Stab = B.d["proj"], B.d["mixT"], B.d["s5_ctab"], B.d["s5_stab"]
    V = nc.vector
    r, L1, L2, CW1, CW2 = ctx["r"], ctx["L1"], ctx["L2"], ctx["CW1"], ctx["CW2"]
    if True:
        uTs = [B.sb(es, "uT", [128, 4, 512], BF16) for _ in range(2)]
        ust = B.sb(es, "ust", [128, 4, 512])
        dsk = B.sb(es, "dsk", [128, 4]); bgl = B.sb(es, "bgl", [128, 4]); nws = B.sb(es, "nws", [128, 4])
        B.dma(dsk[:], B.d["s5_d"][l], writes=[dsk]); B.dma(bgl[:], B.d["s5_b_glu"][l], writes=[bgl]); B.dma(nws[:], B.d["s5_norm_w"][l], writes=[nws])
        wg = [B.sb(es, "wg", [128, 512], BF16) for _ in range(4)]
        wgv = B.d["s5_w_glu"][l].rearrange("(kc p) n -> p kc n", p=128)
        for kc in range(4):
            B.dma(wg[kc][:], wgv[:, kc, :], writes=[wg[kc]], q="pool")
        ones = B.sb(es, "ones", [128, 128], BF16)
        B.op("pool", lambda: nc.gpsimd.memset(ones[:], 1.0), [], [ones])
        wlast = B.sb(es, "wlast", [128, 32])
        NTB = 4
        Ct = [B.sb(es, "Ct", [128, 512]) for _ in range(NTB)]; St = [B.sb(es, "St", [128, 512]) for _ in range(NTB)]
        PA = [B.ps(es, "PA") for _ in range(2)]; PB = [B.ps(es, "PB") for _ in range(2)]
        yps = B.ps(es, "yps"); pz = B.ps(es, "pz"); pss = pz
        t1 = [B.sb(es, "t1", [128, 512]) for _ in range(2)]; t2 = [B.sb(es, "t2", [128, 512]) for _ in range(2)]
        bt = [B.sb(es, "bt", [128, 512]) for _ in range(2)]; wv = [B.sb(es, "w", [128, 512]) for _ in range(2)]
        Wc = [B.sb(es, "Wc", [128, 512], BF16) for _ in range(2)]; Ws = [B.sb(es, "Ws", [128, 512], BF16) for _ in range(2)]
        uf = [B.sb(es, "uf", [128, 512]) for _ in range(2)]
        yg = B.sb(es, "yg", [128, 4, 512]); ygb = B.sb(es, "ygb", [128, 4, 512], BF16)
        sig = B.sb(es, "sig", [128, 512]); o = B.sb(es, "o", [128, 4, 512]); sq = B.sb(es, "sq", [128, 4, 512], BF16)
        rs = B.sb(es, "rs", [128, 512]); rstd = B.sb(es, "rstd", [128, 512]); ob = B.sb(es, "ob", [128, 4, 512], BF16)
        items = [(s, tile, g) for s in range(NS) for tile in range(L // 512) for g in range(32)]

        def front(i):
            s, tile, g = items[i]
            t0 = s * L + tile * 512
            uT = uTs[(s * (L // 512) + tile) % 2]
            if g == 0:
                B.dma(ust[:], proj[R_SU:R_SU + 512, t0:t0 + 512].rearrange("(c p) t -> p c t", p=128), writes=[ust])
                B.op("act", lambda: nc.scalar.copy(uT[:], ust[:]), [ust], [uT])
            B.dma(Ct[i % NTB][:], Ctab[g, :, tile * 512:(tile + 1) * 512], writes=[Ct[i % NTB]])
            B.dma(St[i % NTB][:], Stab[g, :, tile * 512:(tile + 1) * 512], writes=[St[i % NTB]])
            B.op("pe", lambda: nc.tensor.matmul(PA[i % 2][:], L1[:, g, :], uT[:, g // 8, :], start=True, stop=True), [L1, uT], [PA[i % 2]])
            B.op("pe", lambda: nc.tensor.matmul(PB[i % 2][:], L2[:, g, :], uT[:, g // 8, :], start=True, stop=True), [L2, uT], [PB[i % 2]])

        def mid0(i):
            s, tile, g = items[i]
            b = i % 2
            B.op("dve", lambda: V.tensor_tensor(t1[b][:], PA[b][:], Ct[i % NTB][:], ALU.mult), [PA[b], Ct[i % NTB]], [t1[b]])
            B.op("dve", lambda: V.tensor_tensor(t2[b][:], PB[b][:], St[i % NTB][:], ALU.mult), [PB[b], St[i % NTB]], [t2[b]])

        def mid(i):
            s, tile, g = items[i]
            b = i % 2
            B.op("dve", lambda: V.tensor_tensor(bt[b][:], t1[b][:], t2[b][:], ALU.add), [t1[b], t2[b]], [bt[b]])
            init = 0.0 if tile == 0 else wlast[:, g:g + 1]
            B.op("dve", lambda: V.tensor_tensor_scan(wv[b][:], r[:, g:g + 1].to_broadcast([128, 512]), bt[b][:], init, ALU.mult, ALU.add),
                 [r, bt[b], wlast], [wv[b]])
            B.op("act", lambda: nc.scalar.copy(wlast[:, g:g + 1], wv[b][:, 511:512]), [wv[b]], [wlast])
            B.op("pool", lambda: nc.gpsimd.tensor_tensor(Wc[b][:], wv[b][:], Ct[i % NTB][:], ALU.mult), [wv[b], Ct[i % NTB]], [Wc[b]])
            B.op("pool", lambda: nc.gpsimd.tensor_tensor(Ws[b][:], wv[b][:], St[i % NTB][:], ALU.mult), [wv[b], St[i % NTB]], [Ws[b]])

        def back(i):
            s, tile, g = items[i]
            b = i % 2
            j = g % 8
            B.op("pe", lambda: [nc.tensor.matmul(yps[:], CW1[:, g, :], Wc[b][:], start=(j == 0), stop=False),
                                nc.tensor.matmul(yps[:], CW2[:, g, :], Ws[b][:], start=False, stop=(j == 7))], [CW1, CW2, Wc[b], Ws[b]], [yps])
            if j == 7:
                epi(s, tile, g // 8)

        def epi(s, tile, gc):
            t0 = s * L + tile * 512
            U = uf[gc % 2]
            B.dma(U[:], proj[R_SU + gc * 128:R_SU + (gc + 1) * 128, t0:t0 + 512], writes=[U])
            B.op("dve", lambda: V.scalar_tensor_tensor(out=U[:], in0=U[:], scalar=dsk[:, gc:gc + 1], in1=yps[:], op0=ALU.mult, op1=ALU.add), [U, dsk, yps], [U])
            B.op("act", lambda: nc.scalar.activation(yg[:, gc, :], U[:], AF.Gelu), [U], [yg])
            B.op("pool", lambda: nc.gpsimd.tensor_copy(ygb[:, gc, :], yg[:, gc, :]), [yg], [ygb])
            if gc == 3:
                for oc in range(4):
                    B.op("pe", lambda: mm_acc(nc, pz[:], [(wg[kc][:, oc * 128:(oc + 1) * 128], ygb[:, kc, :]) for kc in range(4)]), [ygb] + wg, [pz])
                    B.op("act", lambda: nc.scalar.activation(sig[:], pz[:], AF.Sigmoid, bias=bgl[:, oc:oc + 1]), [pz, bgl], [sig])
                    B.op("dve", lambda: V.tensor_tensor(o[:, oc, :], yg[:, oc, :], sig[:], ALU.mult), [yg, sig], [o])
                B.op("act", lambda: nc.scalar.activation(sq[:], o[:], AF.Square), [o], [sq])
                B.op("pe", lambda: mm_acc(nc, pss[:], [(ones[:], sq[:, kc, :]) for kc in range(4)]), [ones, sq], [pss])
                rms_rstd(B, pss, rs, rstd, 512.0)
                for oc in range(4):
                    B.op("dve", lambda: V.scalar_tensor_tensor(out=ob[:, oc, :], in0=o[:, oc, :], scalar=nws[:, oc:oc + 1], in1=rstd[:],
                                                               op0=ALU.mult, op1=ALU.mult), [o, nws, rstd], [ob])
                B.dma(mixT[512:1024, t0:t0 + 512].rearrange("(c p) t -> p c t", p=128), ob[:], reads=[ob], q="act")

        front(0)
        for i in range(len(items)):
            if i + 1 < len(items):
                front(i + 1)
            mid0(i)
            mid(i)
            back(i)
            yield


def st_outproj(B, l, x_src, x_dst):
    nc = B.nc
    mixT = B.d["mixT"]
    with ExitStack() as es:
        wv = B.d["w_out"][l].rearrange("(kc p) n -> p kc n", p=128)
        W = [B.sb(es, "wo", [128, 1024], BF16) for _ in range(12)]
        for kc in range(12):
            B.dma(W[kc][:], wv[:, kc, :], writes=[W[kc]], q="pool")
        M = [B.sb(es, "M", [128, 12, 512], BF16) for _ in range(2)]
        X = [B.sb(es, "X", [128, 8, 512]) for _ in range(2)]
        pm = [B.ps(es, "pm") for _ in range(4)]
        xv = x_src.rearrange("(kc p) t -> p kc t", p=128)
        xo = x_dst.rearrange("(kc p) t -> p kc t", p=128)
        mv = mixT.rearrange("(kc p) t -> p kc t", p=128)
        for tt in range(B.TOK // 512):
            ts = slice(tt * 512, (tt + 1) * 512)
            Mt, Xt = M[tt % 2], X[tt % 2]
            B.dma(Mt[:], mv[:, :, ts], writes=[Mt])
            B.dma(Xt[:], xv[:, :, ts], writes=[Xt])
            for oc in range(8):
                P = pm[oc % 4]
                B.op("pe", lambda: mm_acc(nc, P[:], [(W[kc][:, oc * 128:(oc + 1) * 128], Mt[:, kc, :]) for kc in range(12)]), [Mt] + W, [P])
                B.op("dve", lambda: nc.vector.tensor_tensor(Xt[:, oc, :], Xt[:, oc, :], P[:], ALU.add), [Xt, P], [Xt])
            B.dma(xo[:, :, ts], Xt[:], reads=[Xt])
    B.S.barrier()


def st_ffn_up(B, l, x_src):
    nc = B.nc
    L = B.L
    gT = B.d["gT"]
    with ExitStack() as es:
        wv = B.d["ffn_w_up"][l].rearrange("(kc p) n -> p kc n", p=128)
        W = [B.sb(es, "wu", [128, 5632], BF16) for _ in range(8)]
        for kc in range(8):
            B.dma(W[kc][:], wv[:, kc, :], writes=[W[kc]], q="pool")
        nw = B.sb(es, "nw", [128, 8]); B.dma(nw[:], B.d["ffn_norm_w"][l], writes=[nw])
        cw = B.sb(es, "cw", [128, 44, 3]); B.dma(cw[:], B.d["ffn_conv_w"][l], writes=[cw])
        cb = B.sb(es, "cb", [128, 44]); B.dma(cb[:], B.d["ffn_conv_b"][l], writes=[cb])
        ones = B.sb(es, "ones", [128, 128], BF16)
        B.op("pool", lambda: nc.gpsimd.memset(ones[:], 1.0), [], [ones])
        xt = [B.sb(es, "xt", [128, 8, 512]) for _ in range(2)]
        hT = [B.sb(es, "hT", [128, 8, 512], BF16) for _ in range(2)]
        sq = B.sb(es, "sq", [128, 8, 512], BF16)
        rs = B.sb(es, "rs", [128, 512]); rstd = B.sb(es, "rstd", [128, 512])
        pss = B.ps(es, "pss")
        pm = [B.ps(es, "pm") for _ in range(4)]
        U = [B.sb(es, "U", [128, 514]) for _ in range(4)]
        tails = [B.sb(es, "tail", [128, 44, 2]) for _ in range(2)]
        cv = [B.sb(es, "cv", [128, 512]) for _ in range(4)]
        sg = [B.sb(es, "sg", [128, 512]) for _ in range(2)]
        G = [B.sb(es, "G", [128, 512], BF16) for _ in range(2)]
        k = 0
        for tt in range(B.TOK // 512):
            X, H = xt[tt % 2], hT[tt % 2]
            ts = slice(tt * 512, (tt + 1) * 512)
            first = (tt * 512) % L == 0
            told, tnew = tails[tt % 2], tails[(tt + 1) % 2]
            if tt == 0:
                load_norm_tile(B, es, x_src, nw, ones, 0, X, H, sq, pss, rs, rstd)
            for fc in range(22):
                if fc == 4 and (tt + 1) * 512 < B.TOK:
                    load_norm_tile(B, es, x_src, nw, ones, tt + 1, xt[(tt + 1) % 2], hT[(tt + 1) % 2], sq, pss, rs, rstd)
                res = []
                for half in range(2):
                    ch = fc + 22 * half
                    P, Ut, C = pm[k % 4], U[k % 4], cv[k % 4]
                    k += 1
                    B.op("pe", lambda: mm_acc(nc, P[:], [(W[kc][:, ch * 128:(ch + 1) * 128], H[:, kc, :]) for kc in range(8)]), [H] + W, [P])
                    B.op("act", lambda: nc.scalar.copy(Ut[:, 2:514], P[:]), [P], [Ut])
                    B.op("act", lambda: nc.scalar.copy(tnew[:, ch, :], P[:, 510:512]), [P], [tnew])
                    B.op("act", lambda: nc.scalar.activation(C[:], P[:], AF.Identity, bias=cb[:, ch:ch + 1], scale=cw[:, ch, 2:3]), [P, cb, cw], [C])
                    if first:
                        B.op("dve", lambda: nc.vector.memset(Ut[:, 0:2], 0.0), [], [Ut])
                    else:
                        B.op("dve", lambda: nc.vector.tensor_copy(Ut[:, 0:2], told[:, ch, :]), [told], [Ut])
                    B.op("dve", lambda: nc.vector.scalar_tensor_tensor(out=C[:], in0=Ut[:, 0:512], scalar=cw[:, ch, 0:1], in1=C[:], op0=ALU.mult, op1=ALU.add), [Ut, cw, C], [C])
                    B.op("dve", lambda: nc.vector.scalar_tensor_tensor(out=C[:], in0=Ut[:, 1:513], scalar=cw[:, ch, 1:2], in1=C[:], op0=ALU.mult, op1=ALU.add), [Ut, cw, C], [C])
                    res.append(C)
                Sg, Gt = sg[fc % 2], G[fc % 2]
                B.op("act", lambda: nc.scalar.activation(Sg[:], res[0][:], AF.Silu), [res[0]], [Sg])
                B.op("dve", lambda: nc.vector.tensor_tensor(Gt[:], Sg[:], res[1][:], ALU.mult), [Sg, res[1]], [Gt])
                B.dma(gT[fc * 128:(fc + 1) * 128, ts], Gt[:], reads=[Gt])
    B.S.barrier()


def st_ffn_down(B, l, x_src, x_dst, final):
    nc = B.nc
    gT = B.d["gT"]
    with ExitStack() as es:
        wv = B.d["ffn_w_down"][l].rearrange("(kc p) n -> p kc n", p=128)
        W = [B.sb(es, "wd", [128, 1024], BF16) for _ in range(22)]
        for kc in range(22):
            B.dma(W[kc][:], wv[:, kc, :], writes=[W[kc]], q="pool")
        Gt = [B.sb(es, "G", [128, 22, 512], BF16) for _ in range(2)]
        X = [B.sb(es, "X", [128, 8, 512]) for _ in range(2)]
        pm = [B.ps(es, "pm") for _ in range(4)]
        xv = x_src.rearrange("(kc p) t -> p kc t", p=128)
        xo = x_dst.rearrange("(kc p) t -> p kc t", p=128)
        gv = gT.rearrange("(kc p) t -> p kc t", p=128)
        if final:
            nw = B.sb(es, "nw", [128, 8]); B.dma(nw[:], B.d["final_norm_w"], writes=[nw])
            ones = B.sb(es, "ones", [128, 128], BF16)
            B.op("pool", lambda: nc.gpsimd.memset(ones[:], 1.0), [], [ones])
            sq = B.sb(es, "sq", [128, 8, 512], BF16)
            rs = B.sb(es, "rs", [128, 512]); rstd = B.sb(es, "rstd", [128, 512])
            pss = B.ps(es, "pss")
            Y = [B.sb(es, "Y", [128, 8, 512]) for _ in range(2)]
        for tt in range(B.TOK // 512):
            ts = slice(tt * 512, (tt + 1) * 512)
            Gc, Xt = Gt[tt % 2], X[tt % 2]
            B.dma(Gc[:], gv[:, :, ts], writes=[Gc])
            B.dma(Xt[:], xv[:, :, ts], writes=[Xt])
            for oc in range(8):
                P = pm[oc % 4]
                B.op("pe", lambda: mm_acc(nc, P[:], [(W[kc][:, oc * 128:(oc + 1) * 128], Gc[:, kc, :]) for kc in range(22)]), [Gc] + W, [P])
                B.op("dve", lambda: nc.vector.tensor_tensor(Xt[:, oc, :], Xt[:, oc, :], P[:], ALU.add), [Xt, P], [Xt])
            if not final:
                B.dma(xo[:, :, ts], Xt[:], reads=[Xt])
            else:
                Yt = Y[tt % 2]
                B.op("act", lambda: nc.scalar.activation(sq[:], Xt[:], AF.Square), [Xt], [sq])
                B.op("pe", lambda: mm_acc(nc, pss[:], [(ones[:], sq[:, kc, :]) for kc in range(8)]), [ones, sq], [pss])
                rms_rstd(B, pss, rs, rstd, 1024.0)
                for kc in range(8):
                    B.op("dve", lambda: nc.vector.scalar_tensor_tensor(out=Yt[:, kc, :], in0=Xt[:, kc, :], scalar=nw[:, kc:kc + 1],
                                                                       in1=rstd[:], op0=ALU.mult, op1=ALU.mult), [Xt, nw, rstd], [Yt])
                B.dma(xo[:, :, ts], Yt[:], reads=[Yt])
    B.S.barrier()


def run_parallel(gens):
    gens = list(gens)
    while gens:
        for item in list(gens):
            g, w = item
            try:
                for _ in range(w):
                    next(g)
            except StopIteration:
                gens.remove(item)


def build(NS, L, dbg=False, depth=DEPTH, stages=None):
    B = Bld(NS, L, dbg, depth)
    TOK = B.TOK
    B.inp("x_in", [1024, TOK])
    B.inp("attn_norm_w", [DEPTH, 128, 8]); B.inp("w_in", [DEPTH, 1024, 4104])
    B.inp("diff_lambda_q1", [DEPTH, 64]); B.inp("diff_lambda_k1", [DEPTH, 64])
    B.inp("diff_lambda_q2", [DEPTH, 64]); B.inp("diff_lambda_k2", [DEPTH, 64])
    B.inp("diff_norm_w", [DEPTH, 128])
    B.inp("gdn_conv_w", [DEPTH, 128, 12, 4]); B.inp("gdn_a_log", [DEPTH, 4]); B.inp("gdn_dt_bias", [DEPTH, 4]); B.inp("gdn_norm_w", [DEPTH, 128, 1])
    B.inp("s5_lre", [DEPTH, 128, 32]); B.inp("s5_lim", [DEPTH, 128, 32]); B.inp("s5_log_dt", [DEPTH, 32])
    B.inp("s5_XA", [DEPTH, 128, 32, 16]); B.inp("s5_XB", [DEPTH, 128, 32, 16])
    B.inp("s5_CA", [DEPTH, 128, 32, 16]); B.inp("s5_CB", [DEPTH, 128, 32, 16])
    B.inp("s5_d", [DEPTH, 128, 4]); B.inp("s5_b_glu", [DEPTH, 128, 4]); B.inp("s5_norm_w", [DEPTH, 128, 4])
    B.inp("s5_w_glu", [DEPTH, 512, 512])
    B.scratch("s5_ctab", [32, 128, L]); B.scratch("s5_stab", [32, 128, L])
    B.inp("w_out", [DEPTH, 1536, 1024]); B.inp("ffn_norm_w", [DEPTH, 128, 8])
    B.inp("ffn_w_up", [DEPTH, 1024, 5632]); B.inp("ffn_conv_w", [DEPTH, 128, 44, 3]); B.inp("ffn_conv_b", [DEPTH, 128, 44])
    B.inp("ffn_w_down", [DEPTH, 2816, 1024]); B.inp("final_norm_w", [128, 8])
    B.scratch("proj", [PROJ_ROWS, TOK]); B.scratch("dv_tm", [TOK, 512], BF16)
    B.scratch("mixT", [1536, TOK], BF16); B.scratch("gT", [2816, TOK], BF16)
    B.scratch("xa", [1024, TOK]); B.scratch("xb", [1024, TOK])
    B.scratch("y", [1024, TOK], out=True)
    x_src = B.d["x_in"]
    for l in range(depth):
        with ExitStack() as esP:
            want = lambda nm: stages is None or nm in stages
            ctx = s5_persist(B, esP) if want("s5") else None
            with ExitStack() as es0:
                gens = []
                if want("inproj"):
                    gens.append((st_inproj(B, l, x_src, es0), 1))
                if want("s5"):
                    gens.append((st_s5_setup(B, l, es0, ctx), 2))
                run_parallel(gens)
                B.S.barrier()
            with ExitStack() as es1:
                gens = []
                if want("gdn"):
                    gens.append((st_gdn(B, l, es1, 8), 1))
                run_parallel(gens)
                B.S.barrier()
            if want("s5"):
                with ExitStack() as es2:
                    run_parallel([(st_s5_main(B, l, es2, ctx), 1)])
                    B.S.barrier()
            if want("attn"):
                with ExitStack() as es3:
                    run_parallel([(st_attn(B, l, es3), 1)])
                    B.S.barrier()
        if stages is None or "outproj" in stages:
            st_outproj(B, l, x_src, B.d["xa"])
        if stages is None or "ffn" in stages:
            st_ffn_up(B, l, B.d["xa"])
            last = l == depth - 1
            st_ffn_down(B, l, B.d["xa"], B.d["y"] if last else B.d["xb"], last)
        x_src = B.d["xb"]
    B.S.finish()
    return B


def prep_weights(inp):
    f = lambda a: np.ascontiguousarray(np.asarray(a, dtype=np.float32))
    w = {}
    perm = np.concatenate([np.arange(0, 2048), np.arange(2056, 2056 + 512 * 4), np.arange(2048, 2056)])
    w["w_in"] = f(inp["w_in"][:, :, perm])
    w["attn_norm_w"] = f(inp["attn_norm_w"].reshape(DEPTH, 8, 128).transpose(0, 2, 1))
    w["ffn_norm_w"] = f(inp["ffn_norm_w"].reshape(DEPTH, 8, 128).transpose(0, 2, 1))
    w["final_norm_w"] = f(inp["final_norm_w"].reshape(8, 128).transpose(1, 0))
    for k in ["diff_lambda_q1", "diff_lambda_k1", "diff_lambda_q2", "diff_lambda_k2", "diff_norm_w", "w_out", "ffn_w_up", "ffn_w_down"]:
        w[k] = f(inp[k])
    dup = lambda a: np.concatenate([a, a], axis=1)
    w["s5_lre"] = f(dup(inp["s5_lambda_re"].transpose(0, 2, 1)))
    w["s5_lim"] = f(dup(inp["s5_lambda_im"].transpose(0, 2, 1)))
    w["s5_log_dt"] = f(inp["s5_log_dt"])
    bre = inp["s5_b_re"].transpose(0, 2, 1, 3); bim = inp["s5_b_im"].transpose(0, 2, 1, 3)
    w["s5_XA"] = f(np.concatenate([bre, bim], axis=1)); w["s5_XB"] = f(np.concatenate([bim, bre], axis=1))
    cre = inp["s5_c_re"].transpose(0, 3, 1, 2); cim = inp["s5_c_im"].transpose(0, 3, 1, 2)
    w["s5_CA"] = f(np.concatenate([cre, cim], axis=1)); w["s5_CB"] = f(np.concatenate([cim, cre], axis=1))
    for k in ["s5_d", "s5_b_glu", "s5_norm_w"]:
        w[k] = f(inp[k].reshape(DEPTH, 4, 128).transpose(0, 2, 1))
    w["s5_w_glu"] = f(inp["s5_w_glu"])
    w["gdn_conv_w"] = f(inp["gdn_conv_w"].reshape(DEPTH, 4, 12, 128).transpose(0, 3, 2, 1))
    w["gdn_a_log"] = f(inp["gdn_a_log"]); w["gdn_dt_bias"] = f(inp["gdn_dt_bias"])
    w["gdn_norm_w"] = f(inp["gdn_norm_w"].reshape(DEPTH, 128, 1))
    w["ffn_conv_w"] = f(inp["ffn_conv_w"].reshape(DEPTH, 3, 44, 128).transpose(0, 3, 2, 1))
    w["ffn_conv_b"] = f(inp["ffn_conv_b"].reshape(DEPTH, 44, 128).transpose(0, 2, 1))
    return w


_CACHE = {}


def kernel(**inputs):
    x = np.asarray(inputs["x"], dtype=np.float32)
    Bsz, L, D = x.shape
    NS = Bsz // N_CORES
    key = (NS, L)
    if key not in _CACHE:
        _CACHE[key] = build(NS, L)
    B = _CACHE[key]
    w = prep_weights(inputs)
    in_maps = []
    for c in range(N_CORES):
        m = dict(w)
        xs = x[c * NS:(c + 1) * NS].reshape(NS * L, D)
        m["x_in"] = np.ascontiguousarray(xs.T)
        in_maps.append(m)
    res = run_bass_kernel_spmd(B.nc, in_maps, core_ids=list(range(N_CORES)))
    out = np.empty((Bsz, L, D), dtype=np.float32)
    for c in range(N_CORES):
        out[c * NS:(c + 1) * NS] = np.asarray(res.results[c]["y"]).T.reshape(NS, L, D)
    return out
```
